# Optimizing a Trainium2 kernel written in Bass

```python
import jax, jax.numpy as jnp
from jax import lax
import numpy as np

D_MODEL = 1024
BATCH = 4
SEQ = 8192
DEPTH = 4

N_A_LAYERS = DEPTH // 2
N_B_LAYERS = DEPTH - N_A_LAYERS
CHUNK = 128
GMLP_WIDTH = 2 * D_MODEL
GMLP_GROUPS = 16
GMLP_GROUP_DIM = GMLP_WIDTH // GMLP_GROUPS
N_HEADS = 16
HEAD_DIM = D_MODEL // N_HEADS
Q_BLOCK = 128
FFN_HIDDEN = -(-8 * D_MODEL // (3 * 256)) * 256
EPS = 1e-6

kernel_name = "yoco_gmlp_fox_adaln_sandwich"


def rms_norm(x, g):
    xf = x.astype(jnp.float32)
    y = xf * lax.rsqrt(jnp.mean(xf * xf, axis=-1, keepdims=True) + EPS)
    return (y * g.astype(jnp.float32)).astype(x.dtype)


def layer_norm(x, g, b):
    xf = x.astype(jnp.float32)
    mu = jnp.mean(xf, axis=-1, keepdims=True)
    xc = xf - mu
    y = xc * lax.rsqrt(jnp.mean(xc * xc, axis=-1, keepdims=True) + EPS)
    return (y * g.astype(jnp.float32) + b.astype(jnp.float32)).astype(x.dtype)


def modulate(h, shift, scale):
    return h * (1 + scale[:, None, :]) + shift[:, None, :]


def swiglu(h, w_gu, w_down):
    gu = h @ w_gu
    g, u = jnp.split(gu, 2, axis=-1)
    return (jax.nn.silu(g) * u) @ w_down


def gmlp_mixer(h, w_in, b_in, ln_g, ln_b, w_s, b_s, w_out):
    B, S, _ = h.shape
    z = jax.nn.gelu(h @ w_in + b_in)
    u, v = jnp.split(z, 2, axis=-1)
    v = layer_norm(v, ln_g, ln_b)
    v = v.reshape(B, S // CHUNK, CHUNK, GMLP_GROUPS, GMLP_GROUP_DIM)
    causal = jnp.tril(jnp.ones((CHUNK, CHUNK), dtype=w_s.dtype))
    ws = w_s * causal[None]
    v = jnp.einsum('gts,bnsgc->bntgc', ws, v) + b_s.T[:, :, None]
    y = u * v.reshape(B, S, GMLP_WIDTH)
    return y @ w_out


def shared_kv(x, mod_kv, kv_norm_g, kv_w, kv_b_f, k_norm_g):
    B, S, _ = x.shape
    shift, scale = jnp.split(mod_kv, 2, axis=-1)
    h = modulate(rms_norm(x, kv_norm_g), shift, scale)
    kvf = h @ kv_w
    k = kvf[..., :D_MODEL].reshape(B, S, N_HEADS, HEAD_DIM)
    k = rms_norm(k, k_norm_g).transpose(0, 2, 1, 3)
    v = kvf[..., D_MODEL:2 * D_MODEL].reshape(B, S, N_HEADS, HEAD_DIM).transpose(0, 2, 1, 3)
    f_logit = kvf[..., 2 * D_MODEL:].astype(jnp.float32) + kv_b_f.astype(jnp.float32)
    dcum = jnp.cumsum(jax.nn.log_sigmoid(f_logit), axis=1).transpose(0, 2, 1)
    return k, v, dcum


def fox_attention(q, k, v, dcum):
    B, H, S, Dh = q.shape
    nb = S // Q_BLOCK
    qb = q.reshape(B, H, nb, Q_BLOCK, Dh).transpose(2, 0, 1, 3, 4)
    db = dcum.reshape(B, H, nb, Q_BLOCK).transpose(2, 0, 1, 3)
    kpos = jnp.arange(S)
    scale = HEAD_DIM ** -0.5

    def block(args):
        qi, di, i = args
        qpos = i * Q_BLOCK + jnp.arange(Q_BLOCK)
        logits = jnp.einsum('bhqd,bhkd->bhqk', qi, k).astype(jnp.float32) * scale
        logits = logits + di[..., :, None] - dcum[..., None, :]
        logits = jnp.where(kpos[None, :] <= qpos[:, None], logits, -jnp.inf)
        p = jax.nn.softmax(logits, axis=-1)
        return jnp.einsum('bhqk,bhkd->bhqd', p.astype(v.dtype), v)

    o = lax.map(block, (qb, db, jnp.arange(nb)))
    return o.transpose(1, 2, 0, 3, 4).reshape(B, H, S, Dh)


def fox_mixer(h, w_qg, q_norm_g, w_o, k, v, dcum):
    B, S, _ = h.shape
    qg = h @ w_qg
    q = qg[..., :D_MODEL].reshape(B, S, N_HEADS, HEAD_DIM)
    q = rms_norm(q, q_norm_g).transpose(0, 2, 1, 3)
    gate = jax.nn.sigmoid(qg[..., D_MODEL:])
    o = fox_attention(q, k, v, dcum).transpose(0, 2, 1, 3).reshape(B, S, D_MODEL)
    return (o * gate) @ w_o


def setup_inputs(seed: int = 0) -> dict:
    key = jax.random.key(seed)
    ks = jax.random.split(key, 26)
    D, F, GW = D_MODEL, FFN_HIDDEN, GMLP_WIDTH

    def nrm(k, shape, scale):
        return jax.random.normal(k, shape, jnp.float32) * scale

    def gain(k, shape):
        return 1.0 + nrm(k, shape, 0.05)

    return {
        "x": nrm(ks[0], (BATCH, SEQ, D), 1.0),
        "c": nrm(ks[1], (BATCH, D), 1.0),
        "ada_w": nrm(ks[2], (DEPTH, D, 6 * D), 0.5 * D ** -0.5),
        "ada_b": nrm(ks[3], (DEPTH, 6 * D), 0.01),
        "pre_mix_g": gain(ks[4], (DEPTH, D)),
        "post_mix_g": gain(ks[5], (DEPTH, D)),
        "pre_ffn_g": gain(ks[6], (DEPTH, D)),
        "post_ffn_g": gain(ks[7], (DEPTH, D)),
        "ffn_w_gu": nrm(ks[8], (DEPTH, D, 2 * F), D ** -0.5),
        "ffn_w_down": nrm(ks[9], (DEPTH, F, D), F ** -0.5),
        "a_w_in": nrm(ks[10], (N_A_LAYERS, D, 2 * GW), D ** -0.5),
        "a_b_in": nrm(ks[11], (N_A_LAYERS, 2 * GW), 0.01),
        "a_ln_g": gain(ks[12], (N_A_LAYERS, GW)),
        "a_ln_b": nrm(ks[13], (N_A_LAYERS, GW), 0.01),
        "a_w_s": nrm(ks[14], (N_A_LAYERS, GMLP_GROUPS, CHUNK, CHUNK), 0.5 * CHUNK ** -0.5),
        "a_b_s": 1.0 + nrm(ks[15], (N_A_LAYERS, GMLP_GROUPS, CHUNK), 0.1),
        "a_w_out": nrm(ks[16], (N_A_LAYERS, GW, D), GW ** -0.5),
        "kv_ada_w": nrm(ks[17], (D, 2 * D), 0.5 * D ** -0.5),
        "kv_ada_b": nrm(ks[18], (2 * D,), 0.01),
        "kv_norm_g": gain(ks[19], (D,)),
        "kv_w": nrm(ks[20], (D, 2 * D + N_HEADS), D ** -0.5),
        "kv_b_f": jax.random.uniform(ks[21], (N_HEADS,), jnp.float32, 1.0, 5.0),
        "k_norm_g": gain(ks[22], (HEAD_DIM,)),
        "b_w_qg": nrm(ks[23], (N_B_LAYERS, D, 2 * D), D ** -0.5),
        "b_q_norm_g": gain(ks[24], (N_B_LAYERS, HEAD_DIM)),
        "b_w_o": nrm(ks[25], (N_B_LAYERS, D, D), D ** -0.5),
    }


def reference(x, c, ada_w, ada_b, pre_mix_g, post_mix_g, pre_ffn_g, post_ffn_g,
              ffn_w_gu, ffn_w_down, a_w_in, a_b_in, a_ln_g, a_ln_b, a_w_s, a_b_s,
              a_w_out, kv_ada_w, kv_ada_b, kv_norm_g, kv_w, kv_b_f, k_norm_g,
              b_w_qg, b_q_norm_g, b_w_o):
    c_act = jax.nn.silu(c)
    k = v = dcum = None
    for layer in range(DEPTH):
        mod = c_act @ ada_w[layer] + ada_b[layer]
        sh_m, sc_m, g_m, sh_f, sc_f, g_f = jnp.split(mod, 6, axis=-1)
        h = modulate(rms_norm(x, pre_mix_g[layer]), sh_m, sc_m)
        if layer < N_A_LAYERS:
            i = layer
            y = gmlp_mixer(h, a_w_in[i], a_b_in[i], a_ln_g[i], a_ln_b[i],
                           a_w_s[i], a_b_s[i], a_w_out[i])
        else:
            j = layer - N_A_LAYERS
            y = fox_mixer(h, b_w_qg[j], b_q_norm_g[j], b_w_o[j], k, v, dcum)
        x = x + g_m[:, None, :] * rms_norm(y, post_mix_g[layer])
        h = modulate(rms_norm(x, pre_ffn_g[layer]), sh_f, sc_f)
        y = swiglu(h, ffn_w_gu[layer], ffn_w_down[layer])
        x = x + g_f[:, None, :] * rms_norm(y, post_ffn_g[layer])
        if layer == N_A_LAYERS - 1:
            k, v, dcum = shared_kv(x, c_act @ kv_ada_w + kv_ada_b, kv_norm_g,
                                   kv_w, kv_b_f, k_norm_g)
    return x
```

```python
import numpy as np
from contextlib import ExitStack
import concourse.bass as bass
import concourse.mybir as mybir
from concourse.bass_utils import run_bass_kernel_spmd

F32 = mybir.dt.float32
BF16 = mybir.dt.bfloat16
AF = mybir.ActivationFunctionType
ALU = mybir.AluOpType
AX = mybir.AxisListType

D = 1024
FH = 2816
GW = 2048
NH = 16
DH = 64
T = 512
EPS = 1e-6
NBUF = 4


class Ctx:
    ENGS = ("pe", "act", "dve", "pool", "sp")

    def __init__(self, nc, es):
        self.nc = nc
        self.es = es
        self.q = {e: [] for e in self.ENGS}
        self.sem = {e: es.enter_context(nc.semaphore("S_" + e)) for e in self.ENGS}
        self.cnt = {e: 0 for e in self.ENGS}
        self.waited = {e: {} for e in self.ENGS}
        self.dsems = {}
        self.dcnt = {}
        self.ninst = 0
        self.pending_dma = []

    def sb(self, name, shape, dt, es=None, tmp=False):
        if tmp:
            return es.enter_context(self.nc.sbuf_tensor(name, shape, dt, side="right"))
        return (es or self.es).enter_context(self.nc.sbuf_tensor(name, shape, dt))

    def barrier(self, prog):
        toks = list(self.pending_dma)
        self.pending_dma = []
        d = prog.dummy
        toks.append(self.op("pool", lambda e: e.memset(d[:, 0:1], 0.0), waits=[prog.t_dummy]))
        toks.append(self.op("dve", lambda e: e.memset(d[:, 1:2], 0.0), waits=[prog.t_dummy]))
        toks.append(self.op("act", lambda e: e.activation(out=d[:, 2:3], in_=d[:, 3:4], func=AF.Copy), waits=[prog.t_dummy]))
        toks.append(self.op("pe", lambda e: e.matmul(prog.bank(0)[:, 0:1], lhsT=prog.ones_f[0:1, 0:128], rhs=prog.ones_f[0:1, 0:1], start=True, stop=True), waits=list(prog.bank_free) + prog.t_ones))
        for e in self.ENGS:
            self.wait(e, toks)
        prog.bank_free = [None] * 8

    def _waits(self, e, waits):
        out = []
        for tok in waits:
            if tok is None:
                continue
            if isinstance(tok, list):
                out.extend(self._waits(e, tok))
                continue
            s, v, key = tok
            if self.waited[e].get(key, 0) >= v:
                continue
            self.waited[e][key] = v
            out.append((s, v))
        return out

    def op(self, e, fn, waits=(), sig=True):
        ws = self._waits(e, waits)
        tok = None
        if sig:
            self.cnt[e] += 1
            tok = (self.sem[e], self.cnt[e], "S_" + e)
        self.ninst += 1

        def run(eng, ws=ws, fn=fn, tok=tok):
            for s, v in ws:
                eng.wait_ge(s, v)
            inst = fn(eng)
            if tok is not None:
                inst.then_inc(tok[0], 1)
        self.q[e].append(run)
        return tok

    def dma(self, e, name, out, in_, waits=(), serial=True, **kw):
        if name not in self.dsems:
            self.dsems[name] = self.es.enter_context(self.nc.semaphore("D_" + name))
            self.dcnt[name] = 0
        waits = list(waits)
        if serial and self.dcnt[name] > 0:
            waits.append((self.dsems[name], self.dcnt[name], "D_" + name))
        ws = self._waits(e, waits)
        self.dcnt[name] += 16
        s = self.dsems[name]
        v = self.dcnt[name]
        self.ninst += 1

        def run(eng, ws=ws):
            for s_, v_ in ws:
                eng.wait_ge(s_, v_)
            eng.dma_start(out=out, in_=in_, **kw).then_inc(s, 16)
        self.q[e].append(run)
        tok = (s, v, "D_" + name)
        self.pending_dma = [t for t in self.pending_dma if t[2] != tok[2]] + [tok]
        return tok

    def wait(self, e, waits):
        ws = self._waits(e, waits)
        if not ws:
            return

        def run(eng, ws=ws):
            for s_, v_ in ws:
                eng.wait_ge(s_, v_)
        self.q[e].append(run)

    def raw(self, e, fn):
        self.q[e].append(fn)

    def flush(self):
        nc = self.nc
        q = self.q
        self.q = {e: [] for e in self.ENGS}
        with nc.Block() as block:
            @block.tensor
            def _(eng):
                for f in q["pe"]:
                    f(eng)

            @block.scalar
            def _(eng):
                for f in q["act"]:
                    f(eng)

            @block.vector
            def _(eng):
                for f in q["dve"]:
                    f(eng)

            @block.gpsimd
            def _(eng):
                for f in q["pool"]:
                    f(eng)

            @block.sync
            def _(eng):
                for f in q["sp"]:
                    f(eng)


def bcast_mid(ap, n):
    a = ap.ap
    return bass.AP(ap.tensor, ap.offset, [list(a[0]), [0, n]] + [list(x) for x in a[1:]])


def bcast_last(ap, n):
    a = ap.ap
    return bass.AP(ap.tensor, ap.offset, [list(x) for x in a] + [[0, n]])


class WStream:
    def __init__(self, c, name, es, nbuf=NBUF, queue="sp"):
        self.c = c
        self.name = name
        self.queue = queue
        self.nbuf = nbuf
        self.slots = [c.sb("%s_w%d" % (name, i), [128, 8, 512], BF16, es) for i in range(nbuf)]
        self.blocks = []
        self.issued = 0
        self.taken = 0
        self.load_tok = {}
        self.rel_tok = {}

    def plan(self, blocks):
        self.blocks = blocks

    def _issue(self):
        i = self.issued
        if i >= len(self.blocks):
            return
        src, nch, fw = self.blocks[i]
        slot = self.slots[i % self.nbuf]
        waits = [fw]
        if i >= self.nbuf:
            waits.append(self.rel_tok[i - self.nbuf])
        self.load_tok[i] = self.c.dma(self.queue, "%s_s%d" % (self.name, i % self.nbuf), slot[:, 0:nch, :], src, waits=waits)
        self.issued += 1

    def start(self):
        for _ in range(self.nbuf):
            self._issue()

    def get(self):
        i = self.taken
        self.taken += 1
        assert i < self.issued, (i, self.issued)
        return i, self.slots[i % self.nbuf], self.load_tok[i]

    def release(self, i, tok):
        self.rel_tok[i] = tok
        while self.issued < len(self.blocks) and (self.issued - self.nbuf) in self.rel_tok:
            self._issue()


class _Stop(Exception):
    pass


class Prog:
    def __init__(self, NT, S, mode):
        self.NT = NT
        self.S = S
        self.NKB = S // 128
        self.mode = mode
        self.nc = bass.Bass("TRN2", target_bir_lowering=False)
        self.es = ExitStack()
        self.c = Ctx(self.nc, self.es)
        nc = self.nc
        self.din = {}
        PS = self.es.enter_context(nc.psum_tensor("PS", [128, 4096], F32))
        self.PS = PS
        self.bank_free = [None] * 8
        self.bank_rr = 0
        import os as _os
        self.stop = int(_os.environ.get("KSTOP", "0"))

    def chk(self, n):
        if self.stop == n:
            raise _Stop()

    def _finish(self, es):
        self.c.barrier(self)
        self.c.flush()
        es.close()

    def inp(self, name, shape, dt=F32):
        t = self.nc.dram_tensor(name, list(shape), dt, kind="ExternalInput").ap()
        self.din[name] = t
        return t

    def outp(self, name, shape, dt=F32):
        return self.nc.dram_tensor(name, list(shape), dt, kind="ExternalOutput").ap()

    def scratch(self, name, shape, dt):
        return self.nc.dram_tensor(name, list(shape), dt).ap()

    def bank(self, i):
        return self.PS[:, i * 512:(i + 1) * 512]

    def next_bank(self):
        i = self.bank_rr
        self.bank_rr = (self.bank_rr + 1) % 8
        return i

    def mmg(self, out_ap, pairs, waits, sig=True):
        c = self.c
        n = len(pairs)
        tok = None
        for i, (l, r) in enumerate(pairs):
            tok = c.op("pe", lambda e, l=l, r=r, i=i: e.matmul(out_ap, lhsT=l, rhs=r, start=(i == 0), stop=(i == n - 1)),
                       waits=waits if i == 0 else (), sig=(sig and i == n - 1))
        return tok

    def conv_b(self, name, w, col0, ncols):
        c = self.c
        nblk = ncols // 512
        dst = self.scratch("wb_" + name, [nblk, 128, 8, 512], BF16)
        for k in range(8):
            src = w[k * 128:(k + 1) * 128, col0:col0 + ncols].rearrange("p (b c) -> p b c", c=512)
            c.dma("pool", "cv_" + name, dst[:, :, k, :].rearrange("b p c -> p b c"), src, serial=False)
        tok = (c.dsems["cv_" + name], c.dcnt["cv_" + name], "D_cv_" + name)
        return [dst[b] for b in range(nblk)], tok

    def conv_gu(self, name, w):
        c = self.c
        dst = self.scratch("wb_" + name, [11, 128, 8, 512], BF16)
        for k in range(8):
            for part in range(2):
                src = w[k * 128:(k + 1) * 128, part * FH:(part + 1) * FH].rearrange("p (b c) -> p b c", c=256)
                c.dma("pool", "cv_" + name, dst[:, :, k, part * 256:(part + 1) * 256].rearrange("b p c -> p b c"), src, serial=False)
        tok = (c.dsems["cv_" + name], c.dcnt["cv_" + name], "D_cv_" + name)
        return [dst[b] for b in range(11)], tok

    def conv_a(self, name, w, nrows):
        c = self.c
        nfc = nrows // 128
        nfg = (nfc + 7) // 8
        dst = self.scratch("wb_" + name, [2, nfg, 128, 8, 512], BF16)
        for fc in range(nfc):
            src = w[fc * 128:(fc + 1) * 128, :].rearrange("p (h c) -> p h c", c=512)
            c.dma("pool", "cv_" + name, dst[:, fc // 8, :, fc % 8, :].rearrange("h p c -> p h c"), src, serial=False)
        tok = (c.dsems["cv_" + name], c.dcnt["cv_" + name], "D_cv_" + name)
        blocks = []
        for hf in range(2):
            for fg in range(nfg):
                nch = min(8, nfc - fg * 8)
                blocks.append((dst[hf, fg, :, 0:nch, :], nch))
        return blocks, tok

    def setup_common(self):
        c = self.c
        ident = c.sb("ident", [128, 128], BF16)
        identf = c.sb("identf", [128, 128], F32)
        t0 = c.op("pool", lambda e: e.memset(identf[:], 1.0))
        t1 = c.op("pool", lambda e: e.affine_select(out=identf[:], in_=identf[:], pattern=[[-1, 128]], compare_op=ALU.is_equal,
                                                      fill=0.0, base=0, channel_multiplier=1), waits=[t0])
        t2 = c.op("pool", lambda e: e.tensor_copy(out=ident[:], in_=identf[:]), waits=[t1])
        self.ident, self.identf, self.t_ident = ident, identf, [t1, t2]
        ones_f = c.sb("ones_f", [128, 128], F32)
        ones_b = c.sb("ones_b", [128, 128], BF16)
        ta = c.op("pool", lambda e: e.memset(ones_f[:], 1.0))
        tb = c.op("pool", lambda e: e.memset(ones_b[:], 1.0))
        self.ones_f, self.ones_b, self.t_ones = ones_f, ones_b, [ta, tb]
        self.dummy = c.sb("dummyt", [128, 4], F32)
        self.t_dummy = c.op("pool", lambda e: e.memset(self.dummy[:], 0.0))
        ccol = c.sb("ccol", [128, 8], F32)
        cact = c.sb("cact", [128, 8], F32)
        td = c.dma("sp", "misc", ccol[:], self.din["c_col"])
        self.t_cact = c.op("act", lambda e: e.activation(out=cact[:], in_=ccol[:], func=AF.Silu), waits=[td])
        self.cact = cact

    def mod_row(self, es, name, w, b_row_ap, ncols, row):
        c = self.c
        nblk = ncols // 512
        wtmp = [c.sb("adaw%d_%s" % (i, name), [128, 8, 512], F32, es, tmp=True) for i in range(2)]
        brow = c.sb("brow_" + name, [1, ncols], F32, es, tmp=True)
        tb = c.dma("sp", "misc", brow[:], b_row_ap)
        rel = [None, None]
        toks = []
        for blk in range(nblk):
            s = blk % 2
            tl = c.dma("sp", "adaw%d" % s, wtmp[s][:], w[:, blk * 512:(blk + 1) * 512].rearrange("(k p) c -> p k c", p=128), waits=[rel[s]])
            bi = self.next_bank()
            out = self.bank(bi)[0:1, :]
            tm = self.mmg(out, [(self.cact[:, k:k + 1], wtmp[s][:, k, :]) for k in range(8)], waits=[tl, self.t_cact, self.bank_free[bi]])
            rel[s] = tm
            te = c.op("dve", lambda e, out=out, blk=blk: e.tensor_tensor(out=row[0:1, blk * 512:(blk + 1) * 512], in0=out,
                                                                        in1=brow[0:1, blk * 512:(blk + 1) * 512], op=ALU.add), waits=[tm, tb])
            self.bank_free[bi] = te
            toks.append(te)
        return toks

    def row_to_cols(self, row_ap, dst, col0, n, waits):
        c = self.c
        bi = self.next_bank()
        bk = self.bank(bi)
        tok = None
        for k in range(n):
            tok = c.op("pe", lambda e, k=k: e.matmul(bk[:, k:k + 1], lhsT=row_ap[0:1, k * 128:(k + 1) * 128], rhs=self.ones_f[0:1, 0:1],
                                                      start=True, stop=True), waits=list(waits) + [self.bank_free[bi]] + self.t_ones if k == 0 else (), sig=(k == n - 1))
        te = c.op("dve", lambda e: e.tensor_copy(out=dst[:, col0:col0 + n], in_=bk[:, 0:n]), waits=[tok])
        self.bank_free[bi] = te
        return te

    def row_to_bcast(self, row_ap, dst, n, waits):
        c = self.c
        toks = []
        for h in range(n // 512):
            bi = self.next_bank()
            bk = self.bank(bi)
            tm = c.op("pe", lambda e, bk=bk, h=h: e.matmul(bk, lhsT=self.ones_f[0:1, 0:128], rhs=row_ap[0:1, h * 512:(h + 1) * 512], start=True, stop=True),
                      waits=list(waits) + [self.bank_free[bi]] + self.t_ones)
            te = c.op("dve", lambda e, bk=bk, h=h: e.tensor_copy(out=dst[:, h * 512:(h + 1) * 512], in_=bk), waits=[tm])
            self.bank_free[bi] = te
            toks.append(te)
        return toks

    def layer_mod_consts(self, es, lname, ada_w_l, ada_b_l, gains, cols, G):
        c = self.c
        row = c.sb("modrow_" + lname, [1, 6 * D], F32, es, tmp=True)
        toks = self.mod_row(es, lname, ada_w_l, ada_b_l, 6 * D, row)
        grow = c.sb("grow_" + lname, [1, 4 * D], F32, es, tmp=True)
        tg = [c.dma("sp", "misc", grow[0:1, i * D:(i + 1) * D], gains[i]) for i in range(4)]
        tmp = c.sb("tmprow_" + lname, [1, 4 * D], F32, es, tmp=True)
        out = []
        for sub in range(2):
            base = sub * 3 * D
            pre = grow[0:1, (2 * sub) * D:(2 * sub + 1) * D]
            post = grow[0:1, (2 * sub + 1) * D:(2 * sub + 2) * D]
            A = tmp[0:1, (2 * sub) * D:(2 * sub + 1) * D]
            Gr = tmp[0:1, (2 * sub + 1) * D:(2 * sub + 2) * D]
            ta = c.op("dve", lambda e, A=A, base=base, pre=pre: e.scalar_tensor_tensor(out=A, in0=row[0:1, base + D:base + 2 * D], scalar=1.0, in1=pre,
                                                                                    op0=ALU.add, op1=ALU.mult), waits=toks + tg)
            tgm = c.op("dve", lambda e, Gr=Gr, base=base, post=post: e.tensor_tensor(out=Gr, in0=row[0:1, base + 2 * D:base + 3 * D], in1=post, op=ALU.mult),
                       waits=toks + tg)
            out.append(self.row_to_cols(A, cols, sub * 16, 8, [ta]))
            out.append(self.row_to_cols(row[0:1, base:base + D], cols, sub * 16 + 8, 8, toks))
            out.extend(self.row_to_bcast(Gr, G[sub], D, [tgm]))
        return out

    def norm_mod(self, x, x_tok, hT, hT_free, cols, c0):
        c = self.c
        st = self.st
        cut = getattr(self, "cut", 99)
        ssq, sq, rstd, xn = st["ssq"], st["sq"], st["rstd"], st["xn"]
        ts = []
        for j in range(4):
            ts.append(c.op("act", lambda e, j=j: e.activation(out=xn[:, j, :], in_=x[:, j, :], func=AF.Square, accum_out=ssq[:, j:j + 1]),
                           waits=[x_tok[j], st.get("xn_free")]))
        if cut == 1:
            return ts
        t1 = c.op("act", lambda e: e.activation(out=sq[:, 0:4], in_=ssq[:, 0:4], func=AF.Sqrt, scale=1.0 / D, bias=st["epsc"][:, 0:1]), waits=ts)
        if cut == 2:
            return [t1]
        t2 = c.op("dve", lambda e: e.reciprocal(out=rstd[:, 0:4], in_=sq[:, 0:4]), waits=[t1])
        if cut == 3:
            return [t2]
        txn = []
        for j in range(4):
            txn.append(c.op("pool", lambda e, j=j: e.tensor_scalar(out=xn[:, j, :], in0=x[:, j, :], scalar1=rstd[:, j:j + 1], scalar2=None, op0=ALU.mult),
                            waits=[t2, x_tok[j], st.get("xn_free")]))
        if cut == 4:
            return txn
        out = []
        last_tp = None
        for kp in range(4):
            bi = self.next_bank()
            bkb = self.bank(bi).bitcast(BF16)
            tok = None
            for kk in range(2):
                k = 2 * kp + kk
                for j in range(4):
                    first = (kk == 0 and j == 0)
                    last = (kk == 1 and j == 3)
                    tok = c.op("pe", lambda e, k=k, j=j, kk=kk, bkb=bkb: e.transpose(out=bkb[:, kk * 512 + j * 128:kk * 512 + (j + 1) * 128],
                                                                                     in_=xn[:, j, k * 128:(k + 1) * 128], identity=self.ident[:]),
                               waits=(txn + [self.bank_free[bi]] + self.t_ident) if first else (), sig=last)
            last_tp = tok
            k0 = 2 * kp
            if cut == 5:
                out.append(tok)
                continue
            ta = c.op("act", lambda e, bkb=bkb, k0=k0: e.activation(out=hT[:, k0, :], in_=bkb[:, 0:512], func=AF.Identity,
                                                                      scale=cols[:, c0 + k0:c0 + k0 + 1], bias=cols[:, c0 + 8 + k0:c0 + 8 + k0 + 1]),
                      waits=[tok, hT_free])
            if cut == 6:
                self.bank_free[bi] = [ta]
                out += [ta]
                continue
            tb = c.op("act", lambda e, bkb=bkb, k0=k0: e.activation(out=hT[:, k0 + 1, :], in_=bkb[:, 512:1024], func=AF.Identity,
                                                                      scale=cols[:, c0 + k0 + 1:c0 + k0 + 2], bias=cols[:, c0 + 8 + k0 + 1:c0 + 8 + k0 + 2]),
                      waits=[tok, hT_free])
            self.bank_free[bi] = [ta, tb]
            out += [ta, tb]
        st["xn_free"] = last_tp
        return out

    def post_resid(self, x, x_tok, ytoks, G):
        c = self.c
        st = self.st
        ssq, sq, rstd = st["ssq2"], st["sq2"], st["rstd2"]
        ts = []
        for j in range(4):
            yj = self.PS[:, j * 1024:(j + 1) * 1024]
            ts.append(c.op("act", lambda e, j=j, yj=yj: e.activation(out=st["xn"][:, j, :], in_=yj, func=AF.Square, accum_out=ssq[:, j:j + 1]), waits=[ytoks[j], st.get("xn_free")]))
        t1 = c.op("act", lambda e: e.activation(out=sq[:, 0:4], in_=ssq[:, 0:4], func=AF.Sqrt, scale=1.0 / D, bias=st["epsc"][:, 0:1]), waits=ts)
        t2 = c.op("dve", lambda e: e.reciprocal(out=rstd[:, 0:4], in_=sq[:, 0:4]), waits=[t1])
        for j in range(4):
            yj = self.PS[:, j * 1024:(j + 1) * 1024]
            tmp = st["rtmp"][j % 2]
            ta = c.op("dve", lambda e, j=j, yj=yj, tmp=tmp: e.scalar_tensor_tensor(out=tmp[:], in0=yj, scalar=rstd[:, j:j + 1], in1=G[:], op0=ALU.mult, op1=ALU.mult),
                      waits=[t2, st["rtmp_free"][j % 2]])
            self.bank_free[2 * j] = ta
            self.bank_free[2 * j + 1] = ta
            tb = c.op("pool", lambda e, j=j, tmp=tmp: e.tensor_tensor(out=x[:, j, :], in0=x[:, j, :], in1=tmp[:], op=ALU.add), waits=[ta, x_tok[j]])
            st["rtmp_free"][j % 2] = tb
            x_tok[j] = tb

    def proj_a(self, ws, actT, nfc, act_toks):
        c = self.c
        nfg = (nfc + 7) // 8
        ytoks = [None] * 4
        for hf in range(2):
            for fg in range(nfg):
                bi_, slot, tl = ws.get()
                nch = min(8, nfc - fg * 8)
                tok = None
                for j in range(4):
                    b = 2 * j + hf
                    for f in range(nch):
                        fc = fg * 8 + f
                        first = (fg == 0 and f == 0)
                        last = (fg == nfg - 1 and f == nch - 1)
                        lastblk = (j == 3 and f == nch - 1)
                        w = []
                        if f == 0:
                            w = [tl] + (list(act_toks) + [self.bank_free[b]] if first else [])
                        tok = c.op("pe", lambda e, b=b, fc=fc, j=j, f=f, slot=slot, first=first, last=last: e.matmul(
                            self.bank(b), lhsT=actT[:, fc, j * 128:(j + 1) * 128], rhs=slot[:, f, :], start=first, stop=last),
                            waits=w, sig=(last or lastblk))
                        if last and hf == 1:
                            ytoks[j] = tok
                ws.release(bi_, tok)
        return ytoks

    def ffn(self, ws, x, x_tok, cols, G):
        c = self.c
        st = self.st
        hT, hid = st["hT"], st["hid"]
        th = self.norm_mod(x, x_tok, hT, st.get("hT_free"), cols, 16)
        hid_toks = []
        last_mm = None
        for blk in range(11):
            bi_, slot, tl = ws.get()
            for pr in range(2):
                fc = 2 * blk + pr
                bg, bu = self.next_bank(), self.next_bank()
                tg = self.mmg(self.bank(bg), [(slot[:, k, pr * 128:(pr + 1) * 128], hT[:, k, :]) for k in range(8)], waits=[tl] + th + [self.bank_free[bg]], sig=True)
                tu = self.mmg(self.bank(bu), [(slot[:, k, 256 + pr * 128:256 + (pr + 1) * 128], hT[:, k, :]) for k in range(8)], waits=[self.bank_free[bu]], sig=True)
                last_mm = tu
                sg = st["sg"][fc % 2]
                ta = c.op("act", lambda e, bg=bg, sg=sg: e.activation(out=sg[:], in_=self.bank(bg), func=AF.Silu), waits=[tg, st["sg_free"][fc % 2]])
                self.bank_free[bg] = ta
                tb = c.op("dve", lambda e, bu=bu, sg=sg, fc=fc: e.tensor_tensor(out=hid[:, fc, :], in0=sg[:], in1=self.bank(bu), op=ALU.mult),
                          waits=[ta, tu, st.get("hid_free")])
                self.bank_free[bu] = tb
                st["sg_free"][fc % 2] = tb
                hid_toks.append(tb)
            ws.release(bi_, last_mm)
        st["hT_free"] = last_mm
        ytoks = self.proj_a(ws, hid, 22, hid_toks)
        st["hid_free"] = ytoks[3]
        self.post_resid(x, x_tok, ytoks, G)

    def gmlp(self, ws, x, x_tok, cols, G, gc):
        c = self.c
        st = self.st
        hT, uT, yT, v = st["hT"], st["uT"], st["yT"], st["v"]
        self.chk(19)
        th = self.norm_mod(x, x_tok, hT, st.get("hT_free"), cols, 0)
        self.chk(20)
        u_toks = []
        tm = None
        for blk in range(4):
            bi_, slot, tl = ws.get()
            for q in range(4):
                fc = blk * 4 + q
                b = self.next_bank()
                tm = self.mmg(self.bank(b), [(slot[:, k, q * 128:(q + 1) * 128], hT[:, k, :]) for k in range(8)], waits=[tl] + th + [self.bank_free[b]])
                ta = c.op("act", lambda e, b=b, fc=fc: e.activation(out=uT[:, fc, :], in_=self.bank(b), func=AF.Gelu_apprx_tanh, bias=gc["bincol"][:, fc:fc + 1]),
                          waits=[tm, st.get("uT_free")])
                self.bank_free[b] = ta
                u_toks.append(ta)
            ws.release(bi_, tm)
        self.chk(21)
        v_toks = [[] for _ in range(4)]
        for cb in range(4):
            bi_, slot, tl = ws.get()
            for j in range(4):
                b = self.next_bank()
                pairs = [(hT[:, k, j * 128:(j + 1) * 128], slot[:, k, :]) for k in range(8)]
                pairs.append((self.ones_b[0:1, 0:128], gc["binrow"][0:1, cb * 512:(cb + 1) * 512]))
                tm = self.mmg(self.bank(b), pairs, waits=[tl] + th + [self.bank_free[b]] + self.t_ones)
                ta = c.op("act", lambda e, b=b, j=j, cb=cb: e.activation(out=v[:, j, cb * 512:(cb + 1) * 512], in_=self.bank(b), func=AF.Gelu_apprx_tanh),
                          waits=[tm, st.get("v_free")])
                self.bank_free[b] = ta
                v_toks[j].append(ta)
            ws.release(bi_, tm)
        st["hT_free"] = tm
        self.chk(22)
        vn_toks = []
        for j in range(4):
            stt = st["bnst"]
            t_s = [c.op("dve", lambda e, j=j, q=q: e.bn_stats(out=stt[:, q, :], in_=v[:, j, q * 512:(q + 1) * 512]), waits=v_toks[j] + [st.get("bn_free")]) for q in range(4)]
            mv = st["mv"]
            t_a = c.op("dve", lambda e: e.bn_aggr(out=mv[:, 0:2], in_=stt[:, :, :]), waits=t_s)
            t_q = c.op("act", lambda e: e.activation(out=mv[:, 2:3], in_=mv[:, 1:2], func=AF.Sqrt, bias=st["epsc"][:, 0:1]), waits=[t_a])
            t_r = c.op("dve", lambda e: e.reciprocal(out=mv[:, 3:4], in_=mv[:, 2:3]), waits=[t_q])
            t_n = c.op("dve", lambda e: e.scalar_tensor_tensor(out=mv[:, 4:5], in0=mv[:, 0:1], scalar=-1.0, in1=mv[:, 3:4], op0=ALU.mult, op1=ALU.mult), waits=[t_r])
            t_1 = c.op("dve", lambda e, j=j: e.tensor_scalar(out=v[:, j, :], in0=v[:, j, :], scalar1=mv[:, 3:4], scalar2=mv[:, 4:5], op0=ALU.mult, op1=ALU.add), waits=[t_n])
            st["bn_free"] = t_1
            t_2 = c.op("pool", lambda e, j=j: e.tensor_tensor(out=v[:, j, :], in0=v[:, j, :], in1=gc["lng"][:], op=ALU.mult), waits=[t_1])
            t_3 = c.op("pool", lambda e, j=j: e.tensor_tensor(out=v[:, j, :], in0=v[:, j, :], in1=gc["lnb"][:], op=ALU.add), waits=[t_2])
            vn_toks.append(t_3)
        self.chk(23)
        y_toks = []
        tm = None
        for g in range(16):
            b = self.next_bank()
            bk = self.bank(b)
            for j in range(4):
                c.op("pe", lambda e, bk=bk, j=j, g=g: e.matmul(bk[:, j * 128:(j + 1) * 128], lhsT=v[:, j, g * 128:(g + 1) * 128], rhs=gc["wsT"][:, g, :], start=True, stop=False),
                     waits=(vn_toks + [self.bank_free[b]]) if j == 0 else (), sig=False)
                tm = c.op("pe", lambda e, bk=bk, j=j, g=g: e.matmul(bk[:, j * 128:(j + 1) * 128], lhsT=self.ones_b[0:2, 0:128], rhs=gc["bs2"][0:2, g * 128:(g + 1) * 128], start=False, stop=True),
                          sig=(j == 3))
            ta = c.op("dve", lambda e, bk=bk, g=g: e.tensor_tensor(out=yT[:, g, :], in0=bk, in1=uT[:, g, :], op=ALU.mult), waits=[tm, u_toks[g], st.get("yT_free")])
            self.bank_free[b] = ta
            y_toks.append(ta)
        st["v_free"] = tm
        st["uT_free"] = y_toks[-1]
        self.chk(24)
        ytoks = self.proj_a(ws, yT, 16, y_toks)
        st["yT_free"] = ytoks[3]
        self.chk(25)
        self.post_resid(x, x_tok, ytoks, G)

    def alloc_state(self, es, big=True):
        c = self.c
        st = {}
        st["ssq"] = c.sb("ssq", [128, 4], F32, es)
        st["sq"] = c.sb("sq", [128, 4], F32, es)
        st["rstd"] = c.sb("rstd", [128, 4], F32, es)
        st["ssq2"] = c.sb("ssq2", [128, 4], F32, es)
        st["sq2"] = c.sb("sq2", [128, 4], F32, es)
        st["rstd2"] = c.sb("rstd2", [128, 4], F32, es)
        st["xn"] = c.sb("xn", [128, 4, 1024], BF16, es)
        st["hT"] = c.sb("hT", [128, 8, 512], BF16, es)
        st["rtmp"] = [c.sb("rtmp%d" % i, [128, 1024], F32, es) for i in range(2)]
        st["rtmp_free"] = [None, None]
        st["sg"] = [c.sb("sg%d" % i, [128, 512], F32, es) for i in range(2)]
        st["sg_free"] = [None, None]
        if big:
            big = c.sb("big", [128, 32, 512], BF16, es)
            st["big"] = big
            st["uT"] = big[:, 0:16, :]
            st["yT"] = big[:, 16:32, :]
            st["hid"] = big[:, 0:22, :]
        st["epsc"] = c.sb("epsc", [128, 1], F32, es)
        st["t_eps"] = c.op("pool", lambda e: e.memset(st["epsc"][:], EPS))
        self.st = st
        return st

    def phase_A(self, x_in, x_mid, kT_d, v_d, lf_d):
        c = self.c
        NT = self.NT
        din = self.din
        es = ExitStack()
        st = self.alloc_state(es)
        wblocks = {}
        for l in range(2):
            ub, t_in = self.conv_b("win%d" % l, din["a_w_in"][l], 0, 4096)
            ob, t_out = self.conv_a("wout%d" % l, din["a_w_out"][l], GW)
            gb, t_gu = self.conv_gu("wgu%d" % l, din["ffn_w_gu"][l])
            db, t_dn = self.conv_a("wdn%d" % l, din["ffn_w_down"][l], FH)
            wblocks[l] = (ub, t_in, ob, t_out, gb, t_gu, db, t_dn)
        kvb, t_kv = self.conv_b("kvw", din["kv_w"], 0, 2048)
        if self.stop == 1:
            return self._finish(es)
        cols = [c.sb("cols%d" % l, [128, 32], F32, es) for l in range(2)]
        Gt = [[c.sb("G%d_%d" % (l, s), [128, 1024], F32, es) for s in range(2)] for l in range(2)]
        ctoks = []
        for l in range(2):
            pes = ExitStack()
            gains = [din[n][l:l + 1, :] for n in ("pre_mix_g", "post_mix_g", "pre_ffn_g", "post_ffn_g")]
            ctoks += self.layer_mod_consts(pes, "L%d" % l, din["ada_w"][l], din["ada_b"][l:l + 1, :], gains, cols[l], Gt[l])
            c.barrier(self)
            c.flush()
            pes.close()
        if self.stop == 2:
            return self._finish(es)
        pes = ExitStack()
        cols_kv = c.sb("cols_kv", [128, 16], F32, es)
        rowkv = c.sb("modrow_kv", [1, 2 * D], F32, pes, tmp=True)
        tkv = self.mod_row(pes, "kv", din["kv_ada_w"], din["kv_ada_b"].rearrange("(o n) -> o n", o=1), 2 * D, rowkv)
        gkv = c.sb("grow_kv", [1, D], F32, pes, tmp=True)
        tgk = c.dma("sp", "misc", gkv[:], din["kv_norm_g"].rearrange("(o n) -> o n", o=1))
        akv = c.sb("arow_kv", [1, D], F32, pes, tmp=True)
        ta = c.op("dve", lambda e: e.scalar_tensor_tensor(out=akv[:], in0=rowkv[0:1, D:2 * D], scalar=1.0, in1=gkv[:], op0=ALU.add, op1=ALU.mult), waits=tkv + [tgk])
        ctoks.append(self.row_to_cols(akv[:], cols_kv, 0, 8, [ta]))
        ctoks.append(self.row_to_cols(rowkv[0:1, 0:D], cols_kv, 8, 8, tkv))
        c.barrier(self)
        c.flush()
        pes.close()
        if self.stop == 3:
            return self._finish(es)
        gcs = []
        for l in range(2):
            pes = ExitStack()
            gc = {}
            gc["bincol"] = c.sb("bincol%d" % l, [128, 16], F32, es)
            ctoks.append(c.dma("sp", "misc", gc["bincol"][:], din["a_b_in"][l, 0:GW].rearrange("(f p) -> p f", p=128), allow_slow_non_contiguous=True))
            binf = c.sb("binf%d" % l, [1, GW], F32, pes, tmp=True)
            t0 = c.dma("sp", "misc", binf[:], din["a_b_in"][l:l + 1, GW:2 * GW])
            gc["binrow"] = c.sb("binrow%d" % l, [1, GW], BF16, es)
            ctoks.append(c.op("dve", lambda e, gc=gc, binf=binf: e.tensor_copy(out=gc["binrow"][:], in_=binf[:]), waits=[t0]))
            for nm, src in (("lng", "a_ln_g"), ("lnb", "a_ln_b")):
                tmpf = c.sb("lnf_%s%d" % (nm, l), [128, GW], F32, pes, tmp=True)
                t0 = c.dma("sp", "misc", tmpf[:], din[src][l].partition_broadcast(128))
                gc[nm] = c.sb("%s%d" % (nm, l), [128, GW], BF16, es)
                ctoks.append(c.op("dve", lambda e, gc=gc, nm=nm, tmpf=tmpf: e.tensor_copy(out=gc[nm][:], in_=tmpf[:]), waits=[t0]))
            wsf = c.sb("wsf%d" % l, [128, 16, 128], F32, pes, tmp=True)
            t0 = c.dma("sp", "misc", wsf[:], din["a_w_s"][l].rearrange("g t s -> t g s"))
            wsm = c.sb("wsm%d" % l, [128, 16, 128], BF16, pes, tmp=True)
            t1 = c.op("pool", lambda e, wsf=wsf: e.affine_select(out=wsf[:], in_=wsf[:], pattern=[[0, 16], [-1, 128]], compare_op=ALU.is_ge, fill=0.0,
                                                                  base=0, channel_multiplier=1), waits=[t0])
            t2 = c.op("pool", lambda e, wsf=wsf, wsm=wsm: e.tensor_copy(out=wsm[:], in_=wsf[:]), waits=[t1])
            gc["wsT"] = c.sb("wsT%d" % l, [128, 16, 128], BF16, es)
            for g4 in range(4):
                bi = self.next_bank()
                bkb = self.bank(bi).bitcast(BF16)
                tok = None
                for q in range(4):
                    g = g4 * 4 + q
                    tok = c.op("pe", lambda e, bkb=bkb, q=q, g=g, wsm=wsm: e.transpose(out=bkb[:, q * 128:(q + 1) * 128], in_=wsm[:, g, :], identity=self.ident[:]),
                               waits=([t2, self.bank_free[bi]] + self.t_ident) if q == 0 else (), sig=(q == 3))
                te = c.op("dve", lambda e, bkb=bkb, g4=g4, gc=gc: e.tensor_copy(out=gc["wsT"][:, g4 * 4:(g4 + 1) * 4, :], in_=bkb[:, 0:512]), waits=[tok])
                self.bank_free[bi] = te
                ctoks.append(te)
            bsf = c.sb("bsf%d" % l, [1, GW], F32, pes, tmp=True)
            t0 = c.dma("sp", "misc", bsf[:], din["a_b_s"][l:l + 1].rearrange("o g t -> o (g t)"))
            gc["bs2"] = c.sb("bs2_%d" % l, [2, GW], BF16, es)
            bslo = c.sb("bslo%d" % l, [1, GW], BF16, pes, tmp=True)
            bsr = c.sb("bsr%d" % l, [1, GW], F32, pes, tmp=True)
            t1 = c.op("dve", lambda e, gc=gc, bsf=bsf: e.tensor_copy(out=gc["bs2"][0:1, :], in_=bsf[:]), waits=[t0])
            t2 = c.op("dve", lambda e, gc=gc, bsf=bsf, bsr=bsr: e.tensor_tensor(out=bsr[:], in0=bsf[:], in1=gc["bs2"][0:1, :], op=ALU.subtract), waits=[t1])
            t3 = c.op("dve", lambda e, bslo=bslo, bsr=bsr: e.tensor_copy(out=bslo[:], in_=bsr[:]), waits=[t2])
            ctoks.append(t1)
            ctoks.append(c.dma("sp", "misc", gc["bs2"][1:2, :], bslo[:], waits=[t3]))
            gcs.append(gc)
            c.barrier(self)
            c.flush()
            pes.close()
        if self.stop == 4:
            return self._finish(es)
        pes = ExitStack()
        bfb = c.sb("bfb", [128, NH], F32, es)
        ctoks.append(c.dma("sp", "misc", bfb[:], din["kv_b_f"].partition_broadcast(128)))
        kgc = c.sb("kgc", [128, 1], F32, es)
        for e2 in range(2):
            ctoks.append(c.dma("sp", "misc", kgc[e2 * 64:(e2 + 1) * 64, :], din["k_norm_g"].rearrange("(p o) -> p o", o=1), allow_slow_non_contiguous=True))
        wf = c.sb("wf", [128, 8, NH], BF16, es)
        ctoks.append(c.dma("pool", "wf", wf[:], din["kv_w"][:, 2 * D:2 * D + NH].rearrange("(k p) n -> p k n", p=128)))
        vsb = c.sb("vsb", [128, 4, NH, DH + 1], BF16, es)
        t_v1 = c.op("pool", lambda e: e.memset(vsb[:], 1.0))
        onesrow = c.sb("onesrow", [NH, NT * 512], BF16, pes, tmp=True)
        t0 = c.op("pool", lambda e: e.memset(onesrow[:], 1.0))
        ctoks.append(c.dma("sp", "misc", kT_d[:, DH, :], onesrow[:], waits=[t0]))
        ctoks += [st["t_eps"], t_v1]
        c.barrier(self)
        c.flush()
        pes.close()

        if self.stop == 5:
            return self._finish(es)
        x = c.sb("xtile", [128, 4, 1024], F32, es)
        v = c.sb("vbuf", [128, 4, GW], BF16, es)
        st["v"] = v
        st["bnst"] = c.sb("bnst", [128, 4, 6], F32, es)
        st["mv"] = c.sb("mv", [128, 8], F32, es)
        kT_sb = c.sb("kT_sb", [128, 8, 512], BF16, es)
        lf_sb = c.sb("lf_sb", [128, 4, NH], F32, es)
        lft = [c.sb("lft%d" % i, [128, 4, NH], F32, es) for i in range(3)]
        hsq = st["rtmp"][0]
        hst = c.sb("hst", [128, 3, NH], F32, es)

        ws = WStream(c, "A", es)
        plan = []
        for t in range(NT):
            for l in range(2):
                ub, t_in, ob, t_out, gb, t_gu, db, t_dn = wblocks[l]
                plan += [(b, 8, t_in) for b in ub]
                plan += [(b, n, t_out) for (b, n) in ob]
                plan += [(b, 8, t_gu) for b in gb]
                plan += [(b, n, t_dn) for (b, n) in db]
            plan += [(b, 8, t_kv) for b in kvb]
        ws.plan(plan)
        for e in ("pe", "act", "dve", "pool", "sp"):
            c.wait(e, ctoks)
        ws.start()

        x_free = None
        kv_dma_free = [None]
        try:
            self._main_A(ws, x, x_in, x_mid, kT_d, v_d, lf_d, cols, Gt, gcs, cols_kv, kT_sb, lf_sb, lft, hsq, hst, vsb, t_v1, bfb, kgc, wf, es)
        except _Stop:
            return self._finish(es)
        return

    def _main_A(self, ws, x, x_in, x_mid, kT_d, v_d, lf_d, cols, Gt, gcs, cols_kv, kT_sb, lf_sb, lft, hsq, hst, vsb, t_v1, bfb, kgc, wf, es):
        c = self.c
        st = self.st
        NT = self.NT
        x_free = None
        kv_dma_free = [None]
        for t in range(NT):
            tl = c.dma("sp", "xload", x[:], x_in[t * 512:(t + 1) * 512, :].rearrange("(j p) d -> p j d", p=128), waits=[x_free])
            x_tok = [tl] * 4
            for l in range(2):
                self.gmlp(ws, x, x_tok, cols[l], Gt[l][0], gcs[l])
                if self.stop == 6:
                    c.dma("sp", "xstore", x_mid[t * 512:(t + 1) * 512, :].rearrange("(j p) d -> p j d", p=128), x[:], waits=list(x_tok))
                    return self._finish(es)
                self.ffn(ws, x, x_tok, cols[l], Gt[l][1])
                if self.stop == 7:
                    c.dma("sp", "xstore", x_mid[t * 512:(t + 1) * 512, :].rearrange("(j p) d -> p j d", p=128), x[:], waits=list(x_tok))
                    return self._finish(es)
            t_xs = c.dma("sp", "xstore", x_mid[t * 512:(t + 1) * 512, :].rearrange("(j p) d -> p j d", p=128), x[:], waits=list(x_tok))
            hT = st["hT"]
            th = self.norm_mod(x, x_tok, hT, st.get("hT_free"), cols_kv, 0)
            x_free = [t_xs, st["xn_free"]]
            ktoks = [None] * 4
            for hf in range(2):
                bi_, slot, tld = ws.get()
                tok = None
                for j in range(4):
                    b = 2 * j + hf
                    tok = self.mmg(self.bank(b), [(hT[:, k, j * 128:(j + 1) * 128], slot[:, k, :]) for k in range(8)], waits=[tld] + th + [self.bank_free[b]])
                    if hf == 1:
                        ktoks[j] = tok
                ws.release(bi_, tok)
            if self.stop == 11:
                return self._finish(es)
            xn = st["xn"]
            kn_toks = []
            for j in range(4):
                kj = self.PS[:, j * 1024:(j + 1) * 1024]
                t1 = c.op("act", lambda e, kj=kj: e.activation(out=hsq[:], in_=kj, func=AF.Square), waits=[ktoks[j], st.get("hsq_free")])
                t2 = c.op("dve", lambda e: e.tensor_reduce(out=hst[:, 0, :], in_=hsq[:].rearrange("p (h d) -> p h d", d=DH), axis=AX.X, op=ALU.add), waits=[t1])
                st["hsq_free"] = t2
                t3 = c.op("act", lambda e: e.activation(out=hst[:, 1, :], in_=hst[:, 0, :], func=AF.Sqrt, scale=1.0 / DH, bias=st["epsc"][:, 0:1]), waits=[t2])
                t4 = c.op("dve", lambda e: e.reciprocal(out=hst[:, 2, :], in_=hst[:, 1, :]), waits=[t3])
                t5 = c.op("dve", lambda e, j=j, kj=kj: e.tensor_tensor(out=xn[:, j, :].rearrange("p (h d) -> p h d", d=DH), in0=kj.rearrange("p (h d) -> p h d", d=DH),
                                                                      in1=bcast_last(hst[:, 2, :], DH), op=ALU.mult), waits=[t4, st.get("xn_free")])
                self.bank_free[2 * j] = t5
                self.bank_free[2 * j + 1] = t5
                kn_toks.append(t5)
            vtoks = [None] * 4
            for hf in range(2):
                bi_, slot, tld = ws.get()
                tok = None
                for j in range(4):
                    b = 2 * j + hf
                    tok = self.mmg(self.bank(b), [(hT[:, k, j * 128:(j + 1) * 128], slot[:, k, :]) for k in range(8)], waits=[tld] + th + [self.bank_free[b]])
                    if hf == 1:
                        vtoks[j] = tok
                ws.release(bi_, tok)
            vs_toks = []
            for j in range(4):
                vj = self.PS[:, j * 1024:(j + 1) * 1024]
                t1 = c.op("act" if j % 2 == 0 else "dve",
                          (lambda e, j=j, vj=vj: e.activation(out=vsb[:, j, :, 0:DH], in_=vj.rearrange("p (h d) -> p h d", d=DH), func=AF.Copy)) if j % 2 == 0 else
                          (lambda e, j=j, vj=vj: e.tensor_copy(out=vsb[:, j, :, 0:DH], in_=vj.rearrange("p (h d) -> p h d", d=DH))),
                          waits=[vtoks[j], kv_dma_free[0], t_v1])
                self.bank_free[2 * j] = t1
                self.bank_free[2 * j + 1] = t1
                vs_toks.append(t1)
            t_vd = [c.dma("sp", "vstore%d" % j, v_d[:, :, t * 4 + j, :].rearrange("h s e -> s h e"), vsb[:, j, :, :], waits=vs_toks) for j in range(4)]
            if self.stop == 12:
                return self._finish(es)
            kt_toks = []
            tp_last = None
            for kp in range(4):
                bi = self.next_bank()
                bkb = self.bank(bi).bitcast(BF16)
                tok = None
                for kk in range(2):
                    k = 2 * kp + kk
                    for j in range(4):
                        first = (kk == 0 and j == 0)
                        tok = c.op("pe", lambda e, k=k, j=j, kk=kk, bkb=bkb: e.transpose(out=bkb[:, kk * 512 + j * 128:kk * 512 + (j + 1) * 128],
                                                                                         in_=xn[:, j, k * 128:(k + 1) * 128], identity=self.ident[:]),
                                   waits=(kn_toks + [self.bank_free[bi]]) if first else (), sig=(kk == 1 and j == 3))
                tp_last = tok
                te = c.op("act", lambda e, bkb=bkb, kp=kp: e.activation(out=kT_sb[:, 2 * kp:2 * kp + 2, :], in_=bkb[:, 0:1024].rearrange("p (k t) -> p k t", t=512),
                                                                          func=AF.Copy, scale=kgc[:, 0:1]), waits=[tok, kv_dma_free[0]])
                self.bank_free[bi] = te
                kt_toks.append(te)
            st["xn_free"] = tp_last
            x_free.append(tp_last)
            t_kd = []
            for e2 in range(2):
                t_kd.append(c.dma("sp", "kstore%d" % e2, kT_d[:, 0:DH, t * 512:(t + 1) * 512].rearrange("(hc e) d t -> e d hc t", e=2)[e2],
                                  kT_sb[e2 * 64:(e2 + 1) * 64, :, :], waits=kt_toks))
            if self.stop == 13:
                return self._finish(es)
            bi = self.next_bank()
            bk = self.bank(bi)
            tok = None
            for j in range(4):
                tok = self.mmg(bk[:, j * NH:(j + 1) * NH], [(hT[:, k, j * 128:(j + 1) * 128], wf[:, k, :]) for k in range(8)], waits=th + [self.bank_free[bi]], sig=(j == 3))
            st["hT_free"] = tok
            fl, fa, fe = lft
            bk3 = bk[:, 0:4 * NH].rearrange("p (j h) -> p j h", h=NH)
            t1 = c.op("dve", lambda e, bk3=bk3: e.tensor_tensor(out=fl[:], in0=bk3, in1=bcast_mid(bfb[:], 4), op=ALU.add), waits=[tok, kv_dma_free[0]])
            self.bank_free[bi] = t1
            t2 = c.op("act", lambda e: e.activation(out=fa[:], in_=fl[:], func=AF.Abs), waits=[t1])
            t3 = c.op("act", lambda e: e.activation(out=fe[:], in_=fa[:], func=AF.Exp, scale=-1.0), waits=[t2])
            t4 = c.op("act", lambda e: e.activation(out=fa[:], in_=fe[:], func=AF.Ln, bias=1.0), waits=[t3])
            t5 = c.op("dve", lambda e: e.tensor_scalar(out=fe[:], in0=fl[:], scalar1=0.0, scalar2=None, op0=ALU.min), waits=[t4])
            t6 = c.op("dve", lambda e: e.tensor_tensor(out=lf_sb[:], in0=fe[:], in1=fa[:], op=ALU.subtract), waits=[t5])
            t_ld = c.dma("sp", "lfstore", lf_d[:, t * 4:(t + 1) * 4, :], lf_sb[:], waits=[t6])
            kv_dma_free[0] = t_vd + [t_ld] + t_kd
        c.wait("sp", [t_xs, kv_dma_free[0]])
        c.barrier(self)
        c.flush()
        es.close()


WEIGHT_SPECS_A = [
    ("ada_w", (2, D, 6 * D)), ("ada_b", (2, 6 * D)), ("pre_mix_g", (2, D)), ("post_mix_g", (2, D)), ("pre_ffn_g", (2, D)), ("post_ffn_g", (2, D)),
    ("ffn_w_gu", (2, D, 2 * FH)), ("ffn_w_down", (2, FH, D)), ("a_w_in", (2, D, 2 * GW)), ("a_b_in", (2, 2 * GW)), ("a_ln_g", (2, GW)), ("a_ln_b", (2, GW)),
    ("a_w_s", (2, 16, 128, 128)), ("a_b_s", (2, 16, 128)), ("a_w_out", (2, GW, D)), ("kv_ada_w", (D, 2 * D)), ("kv_ada_b", (2 * D,)), ("kv_norm_g", (D,)),
    ("kv_w", (D, 2 * D + NH)), ("kv_b_f", (NH,)), ("k_norm_g", (DH,)),
]


def build_A(NT):
    p = Prog(NT, 0, "A")
    p.inp("x_sh", (NT * 512, D))
    p.inp("c_col", (128, 8))
    for n, s in WEIGHT_SPECS_A:
        p.inp(n, s)
    x_mid = p.outp("x_mid", (NT * 512, D))
    kT_d = p.outp("kT_d", (NH, DH + 1, NT * 512), BF16)
    v_d = p.outp("v_d", (NH, 128, NT * 4, DH + 1), BF16)
    lf_d = p.outp("lf_d", (128, NT * 4, NH))
    p.setup_common()
    p.phase_A(p.din["x_sh"], x_mid, kT_d, v_d, lf_d)
    p.es.close()
    return p


def zigzag(nsb, p):
    out = []
    for i in range(nsb // 2):
        out.append(2 * i + ((i + p) % 2))
    return out


def _phase_B(self, x_mid, out_d, kT_all, v_all, lf_all, maskd, flagd):
    c = self.c
    NT = self.NT
    NKB = 8 * NT
    din = self.din
    es = ExitStack()
    st = self.alloc_state(es, big=False)
    sbs = [zigzag(2 * NT, 0), zigzag(2 * NT, 1)]
    wblocks = {}
    for l in range(2):
        qb, t_q = self.conv_b("wq%d" % l, din["b_w_qg"][l], 0, D)
        gb_, t_g = self.conv_b("wg%d" % l, din["b_w_qg"][l], D, D)
        ob, t_o = self.conv_a("wo%d" % l, din["b_w_o"][l], D)
        gu, t_gu = self.conv_gu("wguB%d" % l, din["ffn_w_gu"][l])
        db, t_dn = self.conv_a("wdnB%d" % l, din["ffn_w_down"][l], FH)
        wblocks[l] = (qb, t_q, gb_, t_g, ob, t_o, gu, t_gu, db, t_dn)
    cols = [c.sb("colsB%d" % l, [128, 32], F32, es) for l in range(2)]
    Gt = [[c.sb("GB%d_%d" % (l, s), [128, 1024], F32, es) for s in range(2)] for l in range(2)]
    for l in range(2):
        pes = ExitStack()
        gains = [din[n][l:l + 1, :] for n in ("pre_mix_g", "post_mix_g", "pre_ffn_g", "post_ffn_g")]
        self.layer_mod_consts(pes, "LB%d" % l, din["ada_w"][l], din["ada_b"][l:l + 1, :], gains, cols[l], Gt[l])
        c.barrier(self)
        c.flush()
        pes.close()
    qgc = c.sb("qgc", [64, 2], F32, es)
    t0 = c.dma("sp", "misc", qgc[:], din["b_q_norm_g"].rearrange("l p -> p l"), allow_slow_non_contiguous=True)
    c.op("dve", lambda e: e.tensor_scalar(out=qgc[:], in0=qgc[:], scalar1=DH ** -0.5, scalar2=None, op0=ALU.mult), waits=[t0])
    masks = c.sb("masks", [128, 2, 4, 512], BF16, es)
    c.dma("sp", "misc", masks[:], maskd.rearrange("r k p t -> p r k t"))
    flags = c.sb("flags", [128, 8], F32, es)
    c.dma("sp", "misc", flags[:], flagd)
    pes = ExitStack()
    Dg = c.sb("Dg", [128, NKB, NH], F32, es)
    lfg = c.sb("lfg", [128, NKB, NH], F32, pes, tmp=True)
    sc = [c.sb("scan%d" % i, [128, NKB, NH], F32, pes, tmp=True) for i in range(2)]
    U = c.sb("Utri", [128, 128], F32, pes, tmp=True)
    tl = []
    for r in range(2):
        for li, sbi in enumerate(sbs[r]):
            tl.append(c.dma("sp", "lfl%d" % ((r * NT + li) % 4), lfg[:, 4 * sbi:4 * sbi + 4, :], lf_all[r, :, 4 * li:4 * li + 4, :]))
    tu0 = c.op("pool", lambda e: e.memset(U[:], 1.0))
    tu = c.op("pool", lambda e: e.affine_select(out=U[:], in_=U[:], pattern=[[1, 128]], compare_op=ALU.is_ge, fill=0.0, base=0, channel_multiplier=-1), waits=[tu0])
    cur = lfg
    tprev = tl
    step = 1
    k = 0
    while step < NKB:
        nxt = sc[k % 2]
        ta = c.op("dve", lambda e, cur=cur, nxt=nxt, step=step: e.tensor_copy(out=nxt[:, 0:step, :], in_=cur[:, 0:step, :]), waits=tprev)
        tb = c.op("dve", lambda e, cur=cur, nxt=nxt, step=step: e.tensor_tensor(out=nxt[:, step:NKB, :], in0=cur[:, step:NKB, :], in1=cur[:, 0:NKB - step, :], op=ALU.add), waits=tprev + [ta])
        tprev = [ta, tb]
        cur = nxt
        step *= 2
        k += 1
    ex = sc[k % 2]
    te = c.op("dve", lambda e: e.tensor_tensor(out=ex[:], in0=cur[:], in1=lfg[:], op=ALU.subtract), waits=tprev + tl)
    ncol = NKB * NH
    tds = []
    for h0 in range(0, ncol, 512):
        w = min(512, ncol - h0)
        bi = self.next_bank()
        bk = self.bank(bi)[:, 0:w]
        lf2 = lfg[:].rearrange("p k h -> p (k h)")[:, h0:h0 + w]
        ex2 = ex[:].rearrange("p k h -> p (k h)")[:, h0:h0 + w]
        tm = self.mmg(bk, [(U[:], lf2), (self.ones_f[:], ex2)], waits=[tu, te, self.bank_free[bi]] + tl + self.t_ones)
        td = c.op("dve", lambda e, bk=bk, h0=h0, w=w: e.tensor_copy(out=Dg[:].rearrange("p k h -> p (k h)")[:, h0:h0 + w], in_=bk), waits=[tm])
        self.bank_free[bi] = td
        tds.append(td)
    c.barrier(self)
    c.flush()
    pes.close()

    x = c.sb("xtileB", [128, 4, 1024], F32, es)
    hid = c.sb("hidB", [128, 22, 512], BF16, es)
    st["hid"] = hid
    QT = c.sb("QT", [DH + 1, NH, 512], BF16, es)
    gate = hid[:, 0:8, :].rearrange("p (j a) t -> p j (a t)", a=2)
    og = st["xn"]
    kTb = [c.sb("kTb%d" % i, [DH + 1, 2, NT * 512], BF16, es) for i in range(2)]
    vb = [c.sb("vb%d" % i, [128, 2, NT * 4, DH + 1], BF16, es) for i in range(2)]
    pt = [c.sb("pt%d" % i, [128, 512], BF16, es) for i in range(4)]
    oT = [c.sb("oT%d" % i, [DH + 1, 512], F32, es) for i in range(2)]
    biasT = c.sb("biasT", [128, 2, NT * 4, NH], F32, es)
    Dq = c.sb("Dq", [128, 4, NH], F32, es)
    Dtmp = c.sb("Dtmp", [128, 4, NH], F32, es)
    Drefb = c.sb("Drefb", [128, NH], F32, es)
    cD = c.sb("cD", [128, 4, NH], F32, es)
    crow = c.sb("crow", [NH, 512], BF16, es)
    rinv = c.sb("rinv", [128, 4], F32, es)
    hst = c.sb("hstB", [128, 3, NH], F32, es)
    hsq = st["rtmp"][0]

    ws = WStream(c, "B", es, nbuf=3)
    plan = []
    for t in range(NT):
        for l in range(2):
            qb, t_q, gb_, t_g, ob, t_o, gu, t_gu, db, t_dn = wblocks[l]
            plan += [(b, 8, t_q) for b in qb]
            plan += [(b, 8, t_g) for b in gb_]
            plan += [(b, n, t_o) for (b, n) in ob]
            plan += [(b, 8, t_gu) for b in gu]
            plan += [(b, n, t_dn) for (b, n) in db]
    ws.plan(plan)
    ws.start()

    S_BANKS = [0, 1, 2]
    O_BANKS = [3, 4]
    x_free = None
    kv_free = [None, None]
    pt_free = [None] * 4
    oT_free = [None, None]
    hcount = 0
    ptc = 0
    t_os = None
    for t in range(NT):
        i = t
        tl_ = c.dma("sp", "xloadB", x[:], x_mid[t * 512:(t + 1) * 512, :].rearrange("(j p) d -> p j d", p=128), waits=[x_free])
        x_tok = [tl_] * 4
        g0, g1 = 4 * sbs[0][i], 4 * sbs[1][i]
        t1 = c.op("dve", lambda e, g0=g0: e.tensor_scalar(out=Dtmp[:], in0=Dg[:, g0:g0 + 4, :], scalar1=flags[:, 0:1], scalar2=None, op0=ALU.mult), waits=[st.get("D_free")])
        t2 = c.op("dve", lambda e, g1=g1: e.scalar_tensor_tensor(out=Dq[:], in0=Dg[:, g1:g1 + 4, :], scalar=flags[:, 1:2], in1=Dtmp[:], op0=ALU.mult, op1=ALU.add), waits=[t1])
        bi = self.next_bank()
        bk = self.bank(bi)
        tm = c.op("pe", lambda e, bk=bk: e.matmul(bk[:, 0:NH], lhsT=self.ones_f[0:1, 0:128], rhs=Dq[0:1, 0, :], start=True, stop=True), waits=[t2, self.bank_free[bi]])
        t3 = c.op("dve", lambda e, bk=bk: e.tensor_copy(out=Drefb[:], in_=bk[:, 0:NH]), waits=[tm])
        self.bank_free[bi] = t3
        t4 = c.op("dve", lambda e: e.tensor_tensor(out=cD[:], in0=Dq[:], in1=bcast_mid(Drefb[:], 4), op=ALU.subtract), waits=[t3])
        bi = self.next_bank()
        bk = self.bank(bi)
        tp = None
        for j in range(4):
            tp = c.op("pe", lambda e, bk=bk, j=j: e.transpose(out=bk[0:NH, j * 128:(j + 1) * 128], in_=cD[:, j, :], identity=self.identf[:]),
                      waits=[t4, self.bank_free[bi]] + self.t_ident if j == 0 else (), sig=(j == 3))
        t5 = c.op("dve", lambda e, bk=bk: e.tensor_copy(out=crow[:], in_=bk[0:NH, :]), waits=[tp, st.get("crow_free")])
        self.bank_free[bi] = t5
        t_cr = c.dma("sp", "crowd", QT[DH:DH + 1, :, :], crow[:], waits=[t5, st.get("QT_free")])
        st["crow_free"] = t_cr
        tbias = []
        for r in range(2):
            for li in range(i + 1):
                gk = 4 * sbs[r][li]
                neg = flags[:, 2 + 2 * r + (i % 2):3 + 2 * r + (i % 2)]
                if li == i:
                    tbias.append(c.op("dve", lambda e, r=r, li=li, gk=gk, neg=neg: e.scalar_tensor_tensor(
                        out=biasT[:, r, 4 * li:4 * li + 4, :], in0=bcast_mid(Drefb[:], 4), scalar=neg, in1=Dg[:, gk:gk + 4, :], op0=ALU.add, op1=ALU.subtract),
                        waits=[t3, st.get("bias_free")]))
                else:
                    tbias.append(c.op("dve", lambda e, r=r, li=li, gk=gk: e.tensor_tensor(
                        out=biasT[:, r, 4 * li:4 * li + 4, :], in0=bcast_mid(Drefb[:], 4), in1=Dg[:, gk:gk + 4, :], op=ALU.subtract),
                        waits=[t3, st.get("bias_free")]))
        st["D_free"] = tbias[-1]
        nkl = 4 * (i + 1)
        for l in range(2):
            hT = st["hT"]
            th = self.norm_mod(x, x_tok, hT, st.get("hT_free"), cols[l], 0)
            qtoks = [None] * 4
            for hf in range(2):
                bi_, slot, tld = ws.get()
                tok = None
                for j in range(4):
                    b = 2 * j + hf
                    tok = self.mmg(self.bank(b), [(hT[:, k, j * 128:(j + 1) * 128], slot[:, k, :]) for k in range(8)], waits=[tld] + th + [self.bank_free[b]])
                    if hf == 1:
                        qtoks[j] = tok
                ws.release(bi_, tok)
            xn = st["xn"]
            qn_toks = []
            for j in range(4):
                qj = self.PS[:, j * 1024:(j + 1) * 1024]
                a1 = c.op("act", lambda e, qj=qj: e.activation(out=hsq[:], in_=qj, func=AF.Square), waits=[qtoks[j], st.get("hsq_free"), st["rtmp_free"][0]])
                a2 = c.op("dve", lambda e: e.tensor_reduce(out=hst[:, 0, :], in_=hsq[:].rearrange("p (h d) -> p h d", d=DH), axis=AX.X, op=ALU.add), waits=[a1])
                st["hsq_free"] = a2
                st["rtmp_free"][0] = a2
                a3 = c.op("act", lambda e: e.activation(out=hst[:, 1, :], in_=hst[:, 0, :], func=AF.Sqrt, scale=1.0 / DH, bias=st["epsc"][:, 0:1]), waits=[a2])
                a4 = c.op("dve", lambda e: e.reciprocal(out=hst[:, 2, :], in_=hst[:, 1, :]), waits=[a3])
                a5 = c.op("dve", lambda e, j=j, qj=qj: e.tensor_tensor(out=xn[:, j, :].rearrange("p (h d) -> p h d", d=DH), in0=qj.rearrange("p (h d) -> p h d", d=DH),
                                                                      in1=bcast_last(hst[:, 2, :], DH), op=ALU.mult), waits=[a4, st.get("xn_free")])
                self.bank_free[2 * j] = a5
                self.bank_free[2 * j + 1] = a5
                qn_toks.append(a5)
            gtoks = [None] * 4
            tok = None
            for hf in range(2):
                bi_, slot, tld = ws.get()
                for j in range(4):
                    b = 2 * j + hf
                    tok = self.mmg(self.bank(b), [(hT[:, k, j * 128:(j + 1) * 128], slot[:, k, :]) for k in range(8)], waits=[tld] + th + [self.bank_free[b]])
                    if hf == 1:
                        gtoks[j] = tok
                ws.release(bi_, tok)
            st["hT_free"] = tok
            g_toks = []
            for j in range(4):
                gj = self.PS[:, j * 1024:(j + 1) * 1024]
                a1 = c.op("act", lambda e, j=j, gj=gj: e.activation(out=gate[:, j, :], in_=gj, func=AF.Sigmoid), waits=[gtoks[j], st.get("gate_free")])
                self.bank_free[2 * j] = a1
                self.bank_free[2 * j + 1] = a1
                g_toks.append(a1)
            qt_toks = []
            tp_last = None
            for hp in range(8):
                bi = self.next_bank()
                bkb = self.bank(bi).bitcast(BF16)
                tok = None
                for e2 in range(2):
                    h = 2 * hp + e2
                    for j in range(4):
                        first = (e2 == 0 and j == 0)
                        tok = c.op("pe", lambda e, h=h, j=j, e2=e2, bkb=bkb: e.transpose(out=bkb[0:DH, e2 * 512 + j * 128:e2 * 512 + (j + 1) * 128],
                                                                                         in_=xn[:, j, h * DH:(h + 1) * DH], identity=self.ident[:]),
                                   waits=(qn_toks + [self.bank_free[bi]]) if first else (), sig=(e2 == 1 and j == 3))
                tp_last = tok
                te_ = c.op("act", lambda e, bkb=bkb, hp=hp, l=l: e.activation(out=QT[0:DH, 2 * hp:2 * hp + 2, :], in_=bkb[0:DH, 0:1024].rearrange("p (k t) -> p k t", t=512),
                                                                            func=AF.Copy, scale=qgc[:, l:l + 1]), waits=[tok, st.get("QT_free")])
                self.bank_free[bi] = te_
                qt_toks.append(te_)
            st["xn_free"] = tp_last
            og_toks = []
            last_s = None
            for h in range(NH):
                kb_ = hcount % 2
                hcount += 1
                tk = []
                for r in range(2):
                    tk.append(c.dma("sp", "kld%d_%d" % (kb_, r), kTb[kb_][:, r, 0:nkl * 128], kT_all[r, h, :, 0:nkl * 128], waits=[kv_free[kb_]]))
                    tk.append(c.dma("sp", "vld%d_%d" % (kb_, r), vb[kb_][:, r, 0:nkl, :], v_all[r, h, :, 0:nkl, :], waits=[kv_free[kb_]]))
                ob = O_BANKS[h % 2]
                obk = self.bank(ob)[0:DH + 1, :]
                nblk = 2 * nkl
                pv = None
                idx = 0
                for r in range(2):
                    for kb in range(nkl):
                        sbk = S_BANKS[idx % 3]
                        ts_ = c.op("pe", lambda e, sbk=sbk, kb_=kb_, r=r, kb=kb, h=h: e.matmul(self.bank(sbk), lhsT=kTb[kb_][:, r, kb * 128:(kb + 1) * 128], rhs=QT[:, h, :], start=True, stop=True),
                                   waits=tk + qt_toks + [t_cr, self.bank_free[sbk]] if idx == 0 else [self.bank_free[sbk]])
                        ps_ = ptc % 4
                        ptc += 1
                        if kb >= nkl - 4:
                            ts_ = c.op("dve", lambda e, sbk=sbk, r=r, kk=kb - (nkl - 4): e.tensor_tensor(out=self.bank(sbk), in0=self.bank(sbk), in1=masks[:, r, kk, :], op=ALU.add), waits=[ts_])
                        ta_ = c.op("act", lambda e, sbk=sbk, ps_=ps_, r=r, kb=kb, h=h: e.activation(out=pt[ps_][:], in_=self.bank(sbk), func=AF.Exp, bias=biasT[:, r, kb, h:h + 1]),
                                   waits=[ts_, pt_free[ps_]] + (tbias if idx == 0 else []))
                        self.bank_free[sbk] = ta_
                        tuse = ta_
                        pv = c.op("pe", lambda e, obk=obk, kb_=kb_, r=r, kb=kb, ps_=ps_, idx=idx, nblk=nblk: e.matmul(obk, lhsT=vb[kb_][:, r, kb, :], rhs=pt[ps_][:], start=(idx == 0), stop=(idx == nblk - 1)),
                                  waits=[tuse] + ([self.bank_free[ob]] if idx == 0 else []))
                        pt_free[ps_] = pv
                        idx += 1
                kv_free[kb_] = pv
                last_s = pv
                osl = h % 2
                tcp = c.op("dve", lambda e, obk=obk, osl=osl: e.tensor_copy(out=oT[osl][:], in_=obk), waits=[pv, oT_free[osl]])
                self.bank_free[ob] = tcp
                bi = 5 + (h % 3)
                bk = self.bank(bi)
                tp = None
                for j in range(4):
                    tp = c.op("pe", lambda e, bk=bk, j=j, osl=osl: e.transpose(out=bk[:, j * (DH + 1):(j + 1) * (DH + 1)], in_=oT[osl][:, j * 128:(j + 1) * 128], identity=self.identf[0:DH + 1, 0:DH + 1]),
                              waits=[tcp, self.bank_free[bi]] if j == 0 else (), sig=(j == 3))
                oT_free[osl] = tp
                bk3 = bk[:, 0:4 * (DH + 1)].rearrange("p (j e) -> p j e", e=DH + 1)
                tr = c.op("dve", lambda e, bk3=bk3: e.reciprocal(out=rinv[:].rearrange("p (j o) -> p j o", o=1), in_=bk3[:, :, DH:DH + 1]), waits=[tp, st.get("rinv_free")])
                tg_ = None
                for j in range(4):
                    tg_ = c.op("dve", lambda e, bk3=bk3, j=j, h=h: e.scalar_tensor_tensor(out=og[:, j, h * DH:(h + 1) * DH], in0=bk3[:, j, 0:DH], scalar=rinv[:, j:j + 1],
                                                                                       in1=gate[:, j, h * DH:(h + 1) * DH], op0=ALU.mult, op1=ALU.mult),
                               waits=[tr] + g_toks + [st.get("og_free"), st.get("xn_free")])
                st["rinv_free"] = tg_
                self.bank_free[bi] = tg_
                og_toks.append(tg_)
            st["QT_free"] = last_s
            st["bias_free"] = last_s
            st["gate_free"] = og_toks[-1]
            ogT = st["hT"]
            tt = []
            tp_last = None
            for kp in range(4):
                bi = self.next_bank()
                bkb = self.bank(bi).bitcast(BF16)
                tok = None
                for kk in range(2):
                    k = 2 * kp + kk
                    for j in range(4):
                        first = (kk == 0 and j == 0)
                        tok = c.op("pe", lambda e, k=k, j=j, kk=kk, bkb=bkb: e.transpose(out=bkb[:, kk * 512 + j * 128:kk * 512 + (j + 1) * 128],
                                                                                         in_=og[:, j, k * 128:(k + 1) * 128], identity=self.ident[:]),
                                   waits=(og_toks + [self.bank_free[bi]]) if first else (), sig=(kk == 1 and j == 3))
                tp_last = tok
                te_ = c.op("act", lambda e, bkb=bkb, kp=kp: e.activation(out=ogT[:, 2 * kp:2 * kp + 2, :], in_=bkb[:, 0:1024].rearrange("p (k t) -> p k t", t=512), func=AF.Copy),
                           waits=[tok, st.get("hT_free")])
                self.bank_free[bi] = te_
                tt.append(te_)
            st["og_free"] = tp_last
            st["xn_free"] = tp_last
            ytoks = self.proj_a(ws, ogT, 8, tt)
            st["hT_free"] = ytoks[3]
            self.post_resid(x, x_tok, ytoks, Gt[l][0])
            self.ffn(ws, x, x_tok, cols[l], Gt[l][1])
        t_os = c.dma("sp", "ostore", out_d[t * 512:(t + 1) * 512, :].rearrange("(j p) d -> p j d", p=128), x[:], waits=list(x_tok))
        x_free = [t_os]
    c.wait("sp", [t_os])
    c.barrier(self)
    c.flush()
    es.close()


Prog.phase_B = _phase_B

WEIGHT_SPECS_B = [
    ("ada_w", (2, D, 6 * D)), ("ada_b", (2, 6 * D)), ("pre_mix_g", (2, D)), ("post_mix_g", (2, D)), ("pre_ffn_g", (2, D)), ("post_ffn_g", (2, D)),
    ("ffn_w_gu", (2, D, 2 * FH)), ("ffn_w_down", (2, FH, D)), ("b_w_qg", (2, D, 2 * D)), ("b_q_norm_g", (2, DH)), ("b_w_o", (2, D, D)),
]


def build_B(NT):
    p = Prog(NT, 0, "B")
    p.inp("x_mid", (NT * 512, D))
    p.inp("c_col", (128, 8))
    p.inp("kT_all", (2, NH, DH + 1, NT * 512), BF16)
    p.inp("v_all", (2, NH, 128, NT * 4, DH + 1), BF16)
    p.inp("lf_all", (2, 128, NT * 4, NH))
    p.inp("maskd", (2, 4, 128, 512), BF16)
    p.inp("flagd", (128, 8))
    for n, s in WEIGHT_SPECS_B:
        p.inp(n, s)
    out = p.outp("out", (NT * 512, D))
    p.setup_common()
    p.phase_B(p.din["x_mid"], out, p.din["kT_all"], p.din["v_all"], p.din["lf_all"], p.din["maskd"], p.din["flagd"])
    p.es.close()
    return p


def core_consts(p):
    import ml_dtypes
    m = np.zeros((2, 4, 128, 512), np.float32)
    s = np.arange(128)[:, None]
    t = np.arange(512)[None, :]
    for k in range(4):
        m[p, k] = np.where(t >= 128 * k + s, 0.0, -30000.0)
    f = np.zeros((128, 8), np.float32)
    f[:, p] = 1.0
    NEG = -30000.0
    for par in range(2):
        f[:, 2 + 2 * (1 - p) + par] = 0.0 if ((par + p) % 2 == 1) else NEG
    return m.astype(ml_dtypes.bfloat16), f


_LAYERED = ("ada_w", "ada_b", "pre_mix_g", "post_mix_g", "pre_ffn_g", "post_ffn_g", "ffn_w_gu", "ffn_w_down")
_CACHE = {}


def _tokens(NT, p):
    return np.concatenate([np.arange(q * 512, (q + 1) * 512) for q in zigzag(2 * NT, p)])


def kernel(**inputs):
    inp = {k: np.asarray(v) for k, v in inputs.items()}
    x = inp["x"]
    B, S, _ = x.shape
    NT = S // 1024
    ncores = 2 * B
    toks = [_tokens(NT, 0), _tokens(NT, 1)]
    if ("A", NT) not in _CACHE:
        _CACHE[("A", NT)] = build_A(NT)
    pA = _CACHE[("A", NT)]
    mapsA = []
    for core in range(ncores):
        b, p = core // 2, core % 2
        m = {"x_sh": np.ascontiguousarray(x[b, toks[p]]), "c_col": np.ascontiguousarray(inp["c"][b].reshape(8, 128).T)}
        for n, s in WEIGHT_SPECS_A:
            m[n] = np.ascontiguousarray(inp[n][0:2]) if n in _LAYERED else inp[n]
        mapsA.append(m)
    resA = run_bass_kernel_spmd(pA.nc, mapsA, core_ids=list(range(ncores))).results
    if ("B", NT) not in _CACHE:
        _CACHE[("B", NT)] = build_B(NT)
    pB = _CACHE[("B", NT)]
    mapsB = []
    for core in range(ncores):
        b, p = core // 2, core % 2
        mk, fl = core_consts(p)
        r0, r1 = resA[2 * b], resA[2 * b + 1]
        m = {"x_mid": np.asarray(resA[core]["x_mid"]), "c_col": mapsA[core]["c_col"],
             "kT_all": np.stack([np.asarray(r0["kT_d"]), np.asarray(r1["kT_d"])]),
             "v_all": np.stack([np.asarray(r0["v_d"]), np.asarray(r1["v_d"])]),
             "lf_all": np.stack([np.asarray(r0["lf_d"]), np.asarray(r1["lf_d"])]),
             "maskd": mk, "flagd": fl}
        for n, s in WEIGHT_SPECS_B:
            m[n] = np.ascontiguousarray(inp[n][2:4]) if n in _LAYERED else inp[n]
        mapsB.append(m)
    resB = run_bass_kernel_spmd(pB.nc, mapsB, core_ids=list(range(ncores))).results
    out = np.empty((B, S, D), np.float32)
    for core in range(ncores):
        b, p = core // 2, core % 2
        out[b, toks[p]] = np.asarray(resB[core]["out"])
    return out
```

```python
import numpy as np
from contextlib import ExitStack
import concourse.bass as bass
import concourse.mybir as mybir
from concourse.bass_utils import run_bass_kernel_spmd

F32 = mybir.dt.float32
BF16 = mybir.dt.bfloat16
AF = mybir.ActivationFunctionType
ALU = mybir.AluOpType
AX = mybir.AxisListType

D = 1024
FH = 2816
GW = 2048
NH = 16
DH = 64
T = 512
EPS = 1e-6
NBUF = 4


class Ctx:
    ENGS = ("pe", "act", "dve", "pool", "sp")

    def __init__(self, nc, es):
        self.nc = nc
        self.es = es
        self.q = {e: [] for e in self.ENGS}
        self.sem = {e: es.enter_context(nc.semaphore("S_" + e)) for e in self.ENGS}
        self.cnt = {e: 0 for e in self.ENGS}
        self.waited = {e: {} for e in self.ENGS}
        self.dsems = {}
        self.dcnt = {}
        self.ninst = 0
        self.pending_dma = []

    def sb(self, name, shape, dt, es=None, tmp=False):
        if tmp:
            return es.enter_context(self.nc.sbuf_tensor(name, shape, dt, side="right"))
        return (es or self.es).enter_context(self.nc.sbuf_tensor(name, shape, dt))

    def barrier(self, prog):
        toks = list(self.pending_dma)
        self.pending_dma = []
        d = prog.dummy
        toks.append(self.op("pool", lambda e: e.memset(d[:, 0:1], 0.0), waits=[prog.t_dummy]))
        toks.append(self.op("dve", lambda e: e.memset(d[:, 1:2], 0.0), waits=[prog.t_dummy]))
        toks.append(self.op("act", lambda e: e.activation(out=d[:, 2:3], in_=d[:, 3:4], func=AF.Copy), waits=[prog.t_dummy]))
        toks.append(self.op("pe", lambda e: e.matmul(prog.bank(0)[:, 0:1], lhsT=prog.ones_f[0:1, 0:128], rhs=prog.ones_f[0:1, 0:1], start=True, stop=True), waits=list(prog.bank_free) + prog.t_ones))
        for e in self.ENGS:
            self.wait(e, toks)
        prog.bank_free = [None] * 8

    def _waits(self, e, waits):
        out = []
        for tok in waits:
            if tok is None:
                continue
            if isinstance(tok, list):
                out.extend(self._waits(e, tok))
                continue
            s, v, key = tok
            if self.waited[e].get(key, 0) >= v:
                continue
            self.waited[e][key] = v
            out.append((s, v))
        return out

    def op(self, e, fn, waits=(), sig=True):
        ws = self._waits(e, waits)
        tok = None
        if sig:
            self.cnt[e] += 1
            tok = (self.sem[e], self.cnt[e], "S_" + e)
        self.ninst += 1

        def run(eng, ws=ws, fn=fn, tok=tok):
            for s, v in ws:
                eng.wait_ge(s, v)
            inst = fn(eng)
            if tok is not None:
                inst.then_inc(tok[0], 1)
        self.q[e].append(run)
        return tok

    def dma(self, e, name, out, in_, waits=(), serial=True, **kw):
        if name not in self.dsems:
            self.dsems[name] = self.es.enter_context(self.nc.semaphore("D_" + name))
            self.dcnt[name] = 0
        waits = list(waits)
        if serial and self.dcnt[name] > 0:
            waits.append((self.dsems[name], self.dcnt[name], "D_" + name))
        ws = self._waits(e, waits)
        self.dcnt[name] += 16
        s = self.dsems[name]
        v = self.dcnt[name]
        self.ninst += 1

        def run(eng, ws=ws):
            for s_, v_ in ws:
                eng.wait_ge(s_, v_)
            eng.dma_start(out=out, in_=in_, **kw).then_inc(s, 16)
        self.q[e].append(run)
        tok = (s, v, "D_" + name)
        self.pending_dma = [t for t in self.pending_dma if t[2] != tok[2]] + [tok]
        return tok

    def wait(self, e, waits):
        ws = self._waits(e, waits)
        if not ws:
            return

        def run(eng, ws=ws):
            for s_, v_ in ws:
                eng.wait_ge(s_, v_)
        self.q[e].append(run)

    def raw(self, e, fn):
        self.q[e].append(fn)

    def flush(self):
        nc = self.nc
        q = self.q
        self.q = {e: [] for e in self.ENGS}
        with nc.Block() as block:
            @block.tensor
            def _(eng):
                for f in q["pe"]:
                    f(eng)

            @block.scalar
            def _(eng):
                for f in q["act"]:
                    f(eng)

            @block.vector
            def _(eng):
                for f in q["dve"]:
                    f(eng)

            @block.gpsimd
            def _(eng):
                for f in q["pool"]:
                    f(eng)

            @block.sync
            def _(eng):
                for f in q["sp"]:
                    f(eng)


def bcast_mid(ap, n):
    a = ap.ap
    return bass.AP(ap.tensor, ap.offset, [list(a[0]), [0, n]] + [list(x) for x in a[1:]])


def bcast_last(ap, n):
    a = ap.ap
    return bass.AP(ap.tensor, ap.offset, [list(x) for x in a] + [[0, n]])


class WStream:
    def __init__(self, c, name, es, nbuf=NBUF, queue="sp"):
        self.c = c
        self.name = name
        self.queue = queue
        self.nbuf = nbuf
        self.slots = [c.sb("%s_w%d" % (name, i), [128, 8, 512], BF16, es) for i in range(nbuf)]
        self.blocks = []
        self.issued = 0
        self.taken = 0
        self.load_tok = {}
        self.rel_tok = {}

    def plan(self, blocks):
        self.blocks = blocks

    def _issue(self):
        i = self.issued
        if i >= len(self.blocks):
            return
        src, nch, fw = self.blocks[i]
        slot = self.slots[i % self.nbuf]
        waits = [fw]
        if i >= self.nbuf:
            waits.append(self.rel_tok[i - self.nbuf])
        self.load_tok[i] = self.c.dma(self.queue, "%s_s%d" % (self.name, i % self.nbuf), slot[:, 0:nch, :], src, waits=waits)
        self.issued += 1

    def start(self):
        for _ in range(self.nbuf):
            self._issue()

    def get(self):
        i = self.taken
        self.taken += 1
        assert i < self.issued, (i, self.issued)
        return i, self.slots[i % self.nbuf], self.load_tok[i]

    def release(self, i, tok):
        self.rel_tok[i] = tok
        while self.issued < len(self.blocks) and (self.issued - self.nbuf) in self.rel_tok:
            self._issue()


class _Stop(Exception):
    pass


class Prog:
    def __init__(self, NT, S, mode):
        self.NT = NT
        self.S = S
        self.NKB = S // 128
        self.mode = mode
        self.nc = bass.Bass("TRN2", target_bir_lowering=False)
        self.es = ExitStack()
        self.c = Ctx(self.nc, self.es)
        nc = self.nc
        self.din = {}
        PS = self.es.enter_context(nc.psum_tensor("PS", [128, 4096], F32))
        self.PS = PS
        self.bank_free = [None] * 8
        self.bank_rr = 0
        import os as _os
        self.stop = int(_os.environ.get("KSTOP", "0"))

    def chk(self, n):
        if self.stop == n:
            raise _Stop()

    def _finish(self, es):
        self.c.barrier(self)
        self.c.flush()
        es.close()

    def inp(self, name, shape, dt=F32):
        t = self.nc.dram_tensor(name, list(shape), dt, kind="ExternalInput").ap()
        self.din[name] = t
        return t

    def outp(self, name, shape, dt=F32):
        return self.nc.dram_tensor(name, list(shape), dt, kind="ExternalOutput").ap()

    def scratch(self, name, shape, dt):
        return self.nc.dram_tensor(name, list(shape), dt).ap()

    def bank(self, i):
        return self.PS[:, i * 512:(i + 1) * 512]

    def next_bank(self):
        i = self.bank_rr
        self.bank_rr = (self.bank_rr + 1) % 8
        return i

    def mmg(self, out_ap, pairs, waits, sig=True):
        c = self.c
        n = len(pairs)
        tok = None
        for i, (l, r) in enumerate(pairs):
            tok = c.op("pe", lambda e, l=l, r=r, i=i: e.matmul(out_ap, lhsT=l, rhs=r, start=(i == 0), stop=(i == n - 1)),
                       waits=waits if i == 0 else (), sig=(sig and i == n - 1))
        return tok

    def conv_b(self, name, w, col0, ncols):
        c = self.c
        nblk = ncols // 512
        dst = self.scratch("wb_" + name, [nblk, 128, 8, 512], BF16)
        for k in range(8):
            src = w[k * 128:(k + 1) * 128, col0:col0 + ncols].rearrange("p (b c) -> p b c", c=512)
            c.dma("pool", "cv_" + name, dst[:, :, k, :].rearrange("b p c -> p b c"), src, serial=False)
        tok = (c.dsems["cv_" + name], c.dcnt["cv_" + name], "D_cv_" + name)
        return [dst[b] for b in range(nblk)], tok

    def conv_gu(self, name, w):
        c = self.c
        dst = self.scratch("wb_" + name, [11, 128, 8, 512], BF16)
        for k in range(8):
            for part in range(2):
                src = w[k * 128:(k + 1) * 128, part * FH:(part + 1) * FH].rearrange("p (b c) -> p b c", c=256)
                c.dma("pool", "cv_" + name, dst[:, :, k, part * 256:(part + 1) * 256].rearrange("b p c -> p b c"), src, serial=False)
        tok = (c.dsems["cv_" + name], c.dcnt["cv_" + name], "D_cv_" + name)
        return [dst[b] for b in range(11)], tok

    def conv_a(self, name, w, nrows):
        c = self.c
        nfc = nrows // 128
        nfg = (nfc + 7) // 8
        dst = self.scratch("wb_" + name, [2, nfg, 128, 8, 512], BF16)
        for fc in range(nfc):
            src = w[fc * 128:(fc + 1) * 128, :].rearrange("p (h c) -> p h c", c=512)
            c.dma("pool", "cv_" + name, dst[:, fc // 8, :, fc % 8, :].rearrange("h p c -> p h c"), src, serial=False)
        tok = (c.dsems["cv_" + name], c.dcnt["cv_" + name], "D_cv_" + name)
        blocks = []
        for hf in range(2):
            for fg in range(nfg):
                nch = min(8, nfc - fg * 8)
                blocks.append((dst[hf, fg, :, 0:nch, :], nch))
        return blocks, tok

    def setup_common(self):
        c = self.c
        ident = c.sb("ident", [128, 128], BF16)
        identf = c.sb("identf", [128, 128], F32)
        t0 = c.op("pool", lambda e: e.memset(identf[:], 1.0))
        t1 = c.op("pool", lambda e: e.affine_select(out=identf[:], in_=identf[:], pattern=[[-1, 128]], compare_op=ALU.is_equal,
                                                      fill=0.0, base=0, channel_multiplier=1), waits=[t0])
        t2 = c.op("pool", lambda e: e.tensor_copy(out=ident[:], in_=identf[:]), waits=[t1])
        self.ident, self.identf, self.t_ident = ident, identf, [t1, t2]
        ones_f = c.sb("ones_f", [128, 128], F32)
        ones_b = c.sb("ones_b", [128, 128], BF16)
        ta = c.op("pool", lambda e: e.memset(ones_f[:], 1.0))
        tb = c.op("pool", lambda e: e.memset(ones_b[:], 1.0))
        self.ones_f, self.ones_b, self.t_ones = ones_f, ones_b, [ta, tb]
        self.dummy = c.sb("dummyt", [128, 4], F32)
        self.t_dummy = c.op("pool", lambda e: e.memset(self.dummy[:], 0.0))
        ccol = c.sb("ccol", [128, 8], F32)
        cact = c.sb("cact", [128, 8], F32)
        td = c.dma("sp", "misc", ccol[:], self.din["c_col"])
        self.t_cact = c.op("act", lambda e: e.activation(out=cact[:], in_=ccol[:], func=AF.Silu), waits=[td])
        self.cact = cact

    def mod_row(self, es, name, w, b_row_ap, ncols, row):
        c = self.c
        nblk = ncols // 512
        wtmp = [c.sb("adaw%d_%s" % (i, name), [128, 8, 512], F32, es, tmp=True) for i in range(2)]
        brow = c.sb("brow_" + name, [1, ncols], F32, es, tmp=True)
        tb = c.dma("sp", "misc", brow[:], b_row_ap)
        rel = [None, None]
        toks = []
        for blk in range(nblk):
            s = blk % 2
            tl = c.dma("sp", "adaw%d" % s, wtmp[s][:], w[:, blk * 512:(blk + 1) * 512].rearrange("(k p) c -> p k c", p=128), waits=[rel[s]])
            bi = self.next_bank()
            out = self.bank(bi)[0:1, :]
            tm = self.mmg(out, [(self.cact[:, k:k + 1], wtmp[s][:, k, :]) for k in range(8)], waits=[tl, self.t_cact, self.bank_free[bi]])
            rel[s] = tm
            te = c.op("dve", lambda e, out=out, blk=blk: e.tensor_tensor(out=row[0:1, blk * 512:(blk + 1) * 512], in0=out,
                                                                        in1=brow[0:1, blk * 512:(blk + 1) * 512], op=ALU.add), waits=[tm, tb])
            self.bank_free[bi] = te
            toks.append(te)
        return toks

    def row_to_cols(self, row_ap, dst, col0, n, waits):
        c = self.c
        bi = self.next_bank()
        bk = self.bank(bi)
        tok = None
        for k in range(n):
            tok = c.op("pe", lambda e, k=k: e.matmul(bk[:, k:k + 1], lhsT=row_ap[0:1, k * 128:(k + 1) * 128], rhs=self.ones_f[0:1, 0:1],
                                                      start=True, stop=True), waits=list(waits) + [self.bank_free[bi]] + self.t_ones if k == 0 else (), sig=(k == n - 1))
        te = c.op("dve", lambda e: e.tensor_copy(out=dst[:, col0:col0 + n], in_=bk[:, 0:n]), waits=[tok])
        self.bank_free[bi] = te
        return te

    def row_to_bcast(self, row_ap, dst, n, waits):
        c = self.c
        toks = []
        for h in range(n // 512):
            bi = self.next_bank()
            bk = self.bank(bi)
            tm = c.op("pe", lambda e, bk=bk, h=h: e.matmul(bk, lhsT=self.ones_f[0:1, 0:128], rhs=row_ap[0:1, h * 512:(h + 1) * 512], start=True, stop=True),
                      waits=list(waits) + [self.bank_free[bi]] + self.t_ones)
            te = c.op("dve", lambda e, bk=bk, h=h: e.tensor_copy(out=dst[:, h * 512:(h + 1) * 512], in_=bk), waits=[tm])
            self.bank_free[bi] = te
            toks.append(te)
        return toks

    def layer_mod_consts(self, es, lname, ada_w_l, ada_b_l, gains, cols, G):
        c = self.c
        row = c.sb("modrow_" + lname, [1, 6 * D], F32, es, tmp=True)
        toks = self.mod_row(es, lname, ada_w_l, ada_b_l, 6 * D, row)
        grow = c.sb("grow_" + lname, [1, 4 * D], F32, es, tmp=True)
        tg = [c.dma("sp", "misc", grow[0:1, i * D:(i + 1) * D], gains[i]) for i in range(4)]
        tmp = c.sb("tmprow_" + lname, [1, 4 * D], F32, es, tmp=True)
        out = []
        for sub in range(2):
            base = sub * 3 * D
            pre = grow[0:1, (2 * sub) * D:(2 * sub + 1) * D]
            post = grow[0:1, (2 * sub + 1) * D:(2 * sub + 2) * D]
            A = tmp[0:1, (2 * sub) * D:(2 * sub + 1) * D]
            Gr = tmp[0:1, (2 * sub + 1) * D:(2 * sub + 2) * D]
            ta = c.op("dve", lambda e, A=A, base=base, pre=pre: e.scalar_tensor_tensor(out=A, in0=row[0:1, base + D:base + 2 * D], scalar=1.0, in1=pre,
                                                                                    op0=ALU.add, op1=ALU.mult), waits=toks + tg)
            tgm = c.op("dve", lambda e, Gr=Gr, base=base, post=post: e.tensor_tensor(out=Gr, in0=row[0:1, base + 2 * D:base + 3 * D], in1=post, op=ALU.mult),
                       waits=toks + tg)
            out.append(self.row_to_cols(A, cols, sub * 16, 8, [ta]))
            out.append(self.row_to_cols(row[0:1, base:base + D], cols, sub * 16 + 8, 8, toks))
            out.extend(self.row_to_bcast(Gr, G[sub], D, [tgm]))
        return out

    def norm_mod(self, x, x_tok, hT, hT_free, cols, c0):
        c = self.c
        st = self.st
        cut = getattr(self, "cut", 99)
        ssq, sq, rstd, xn = st["ssq"], st["sq"], st["rstd"], st["xn"]
        ts = []
        for j in range(4):
            ts.append(c.op("act", lambda e, j=j: e.activation(out=xn[:, j, :], in_=x[:, j, :], func=AF.Square, accum_out=ssq[:, j:j + 1]),
                           waits=[x_tok[j], st.get("xn_free")]))
        if cut == 1:
            return ts
        t1 = c.op("act", lambda e: e.activation(out=sq[:, 0:4], in_=ssq[:, 0:4], func=AF.Sqrt, scale=1.0 / D, bias=st["epsc"][:, 0:1]), waits=ts)
        if cut == 2:
            return [t1]
        t2 = c.op("dve", lambda e: e.reciprocal(out=rstd[:, 0:4], in_=sq[:, 0:4]), waits=[t1])
        if cut == 3:
            return [t2]
        txn = []
        for j in range(4):
            txn.append(c.op("pool", lambda e, j=j: e.tensor_scalar(out=xn[:, j, :], in0=x[:, j, :], scalar1=rstd[:, j:j + 1], scalar2=None, op0=ALU.mult),
                            waits=[t2, x_tok[j], st.get("xn_free")]))
        if cut == 4:
            return txn
        out = []
        last_tp = None
        for kp in range(4):
            bi = self.next_bank()
            bkb = self.bank(bi).bitcast(BF16)
            tok = None
            for kk in range(2):
                k = 2 * kp + kk
                for j in range(4):
                    first = (kk == 0 and j == 0)
                    last = (kk == 1 and j == 3)
                    tok = c.op("pe", lambda e, k=k, j=j, kk=kk, bkb=bkb: e.transpose(out=bkb[:, kk * 512 + j * 128:kk * 512 + (j + 1) * 128],
                                                                                     in_=xn[:, j, k * 128:(k + 1) * 128], identity=self.ident[:]),
                               waits=(txn + [self.bank_free[bi]] + self.t_ident) if first else (), sig=last)
            last_tp = tok
            k0 = 2 * kp
            if cut == 5:
                out.append(tok)
                continue
            ta = c.op("act", lambda e, bkb=bkb, k0=k0: e.activation(out=hT[:, k0, :], in_=bkb[:, 0:512], func=AF.Identity,
                                                                      scale=cols[:, c0 + k0:c0 + k0 + 1], bias=cols[:, c0 + 8 + k0:c0 + 8 + k0 + 1]),
                      waits=[tok, hT_free])
            if cut == 6:
                self.bank_free[bi] = [ta]
                out += [ta]
                continue
            tb = c.op("act", lambda e, bkb=bkb, k0=k0: e.activation(out=hT[:, k0 + 1, :], in_=bkb[:, 512:1024], func=AF.Identity,
                                                                      scale=cols[:, c0 + k0 + 1:c0 + k0 + 2], bias=cols[:, c0 + 8 + k0 + 1:c0 + 8 + k0 + 2]),
                      waits=[tok, hT_free])
            self.bank_free[bi] = [ta, tb]
            out += [ta, tb]
        st["xn_free"] = last_tp
        return out

    def post_resid(self, x, x_tok, ytoks, G):
        c = self.c
        st = self.st
        ssq, sq, rstd = st["ssq2"], st["sq2"], st["rstd2"]
        ts = []
        for j in range(4):
            yj = self.PS[:, j * 1024:(j + 1) * 1024]
            ts.append(c.op("act", lambda e, j=j, yj=yj: e.activation(out=st["xn"][:, j, :], in_=yj, func=AF.Square, accum_out=ssq[:, j:j + 1]), waits=[ytoks[j], st.get("xn_free")]))
        t1 = c.op("act", lambda e: e.activation(out=sq[:, 0:4], in_=ssq[:, 0:4], func=AF.Sqrt, scale=1.0 / D, bias=st["epsc"][:, 0:1]), waits=ts)
        t2 = c.op("dve", lambda e: e.reciprocal(out=rstd[:, 0:4], in_=sq[:, 0:4]), waits=[t1])
        for j in range(4):
            yj = self.PS[:, j * 1024:(j + 1) * 1024]
            tmp = st["rtmp"][j % 2]
            ta = c.op("dve", lambda e, j=j, yj=yj, tmp=tmp: e.scalar_tensor_tensor(out=tmp[:], in0=yj, scalar=rstd[:, j:j + 1], in1=G[:], op0=ALU.mult, op1=ALU.mult),
                      waits=[t2, st["rtmp_free"][j % 2]])
            self.bank_free[2 * j] = ta
            self.bank_free[2 * j + 1] = ta
            tb = c.op("pool", lambda e, j=j, tmp=tmp: e.tensor_tensor(out=x[:, j, :], in0=x[:, j, :], in1=tmp[:], op=ALU.add), waits=[ta, x_tok[j]])
            st["rtmp_free"][j % 2] = tb
            x_tok[j] = tb

    def proj_a(self, ws, actT, nfc, act_toks):
        c = self.c
        nfg = (nfc + 7) // 8
        ytoks = [None] * 4
        for hf in range(2):
            for fg in range(nfg):
                bi_, slot, tl = ws.get()
                nch = min(8, nfc - fg * 8)
                tok = None
                for j in range(4):
                    b = 2 * j + hf
                    for f in range(nch):
                        fc = fg * 8 + f
                        first = (fg == 0 and f == 0)
                        last = (fg == nfg - 1 and f == nch - 1)
                        lastblk = (j == 3 and f == nch - 1)
                        w = []
                        if f == 0:
                            w = [tl] + (list(act_toks) + [self.bank_free[b]] if first else [])
                        tok = c.op("pe", lambda e, b=b, fc=fc, j=j, f=f, slot=slot, first=first, last=last: e.matmul(
                            self.bank(b), lhsT=actT[:, fc, j * 128:(j + 1) * 128], rhs=slot[:, f, :], start=first, stop=last),
                            waits=w, sig=(last or lastblk))
                        if last and hf == 1:
                            ytoks[j] = tok
                ws.release(bi_, tok)
        return ytoks

    def ffn(self, ws, x, x_tok, cols, G):
        c = self.c
        st = self.st
        hT, hid = st["hT"], st["hid"]
        th = self.norm_mod(x, x_tok, hT, st.get("hT_free"), cols, 16)
        hid_toks = []
        last_mm = None
        for blk in range(11):
            bi_, slot, tl = ws.get()
            for pr in range(2):
                fc = 2 * blk + pr
                bg, bu = self.next_bank(), self.next_bank()
                tg = self.mmg(self.bank(bg), [(slot[:, k, pr * 128:(pr + 1) * 128], hT[:, k, :]) for k in range(8)], waits=[tl] + th + [self.bank_free[bg]], sig=True)
                tu = self.mmg(self.bank(bu), [(slot[:, k, 256 + pr * 128:256 + (pr + 1) * 128], hT[:, k, :]) for k in range(8)], waits=[self.bank_free[bu]], sig=True)
                last_mm = tu
                sg = st["sg"][fc % 2]
                ta = c.op("act", lambda e, bg=bg, sg=sg: e.activation(out=sg[:], in_=self.bank(bg), func=AF.Silu), waits=[tg, st["sg_free"][fc % 2]])
                self.bank_free[bg] = ta
                tb = c.op("dve", lambda e, bu=bu, sg=sg, fc=fc: e.tensor_tensor(out=hid[:, fc, :], in0=sg[:], in1=self.bank(bu), op=ALU.mult),
                          waits=[ta, tu, st.get("hid_free")])
                self.bank_free[bu] = tb
                st["sg_free"][fc % 2] = tb
                hid_toks.append(tb)
            ws.release(bi_, last_mm)
        st["hT_free"] = last_mm
        ytoks = self.proj_a(ws, hid, 22, hid_toks)
        st["hid_free"] = ytoks[3]
        self.post_resid(x, x_tok, ytoks, G)

    def gmlp(self, ws, x, x_tok, cols, G, gc):
        c = self.c
        st = self.st
        hT, uT, yT, v = st["hT"], st["uT"], st["yT"], st["v"]
        self.chk(19)
        th = self.norm_mod(x, x_tok, hT, st.get("hT_free"), cols, 0)
        self.chk(20)
        u_toks = []
        tm = None
        for blk in range(4):
            bi_, slot, tl = ws.get()
            for q in range(4):
                fc = blk * 4 + q
                b = self.next_bank()
                tm = self.mmg(self.bank(b), [(slot[:, k, q * 128:(q + 1) * 128], hT[:, k, :]) for k in range(8)], waits=[tl] + th + [self.bank_free[b]])
                ta = c.op("act", lambda e, b=b, fc=fc: e.activation(out=uT[:, fc, :], in_=self.bank(b), func=AF.Gelu_apprx_tanh, bias=gc["bincol"][:, fc:fc + 1]),
                          waits=[tm, st.get("uT_free")])
                self.bank_free[b] = ta
                u_toks.append(ta)
            ws.release(bi_, tm)
        self.chk(21)
        v_toks = [[] for _ in range(4)]
        for cb in range(4):
            bi_, slot, tl = ws.get()
            for j in range(4):
                b = self.next_bank()
                pairs = [(hT[:, k, j * 128:(j + 1) * 128], slot[:, k, :]) for k in range(8)]
                pairs.append((self.ones_b[0:1, 0:128], gc["binrow"][0:1, cb * 512:(cb + 1) * 512]))
                tm = self.mmg(self.bank(b), pairs, waits=[tl] + th + [self.bank_free[b]] + self.t_ones)
                ta = c.op("act", lambda e, b=b, j=j, cb=cb: e.activation(out=v[:, j, cb * 512:(cb + 1) * 512], in_=self.bank(b), func=AF.Gelu_apprx_tanh),
                          waits=[tm, st.get("v_free")])
                self.bank_free[b] = ta
                v_toks[j].append(ta)
            ws.release(bi_, tm)
        st["hT_free"] = tm
        self.chk(22)
        vn_toks = []
        for j in range(4):
            stt = st["bnst"]
            t_s = [c.op("dve", lambda e, j=j, q=q: e.bn_stats(out=stt[:, q, :], in_=v[:, j, q * 512:(q + 1) * 512]), waits=v_toks[j] + [st.get("bn_free")]) for q in range(4)]
            mv = st["mv"]
            t_a = c.op("dve", lambda e: e.bn_aggr(out=mv[:, 0:2], in_=stt[:, :, :]), waits=t_s)
            t_q = c.op("act", lambda e: e.activation(out=mv[:, 2:3], in_=mv[:, 1:2], func=AF.Sqrt, bias=st["epsc"][:, 0:1]), waits=[t_a])
            t_r = c.op("dve", lambda e: e.reciprocal(out=mv[:, 3:4], in_=mv[:, 2:3]), waits=[t_q])
            t_n = c.op("dve", lambda e: e.scalar_tensor_tensor(out=mv[:, 4:5], in0=mv[:, 0:1], scalar=-1.0, in1=mv[:, 3:4], op0=ALU.mult, op1=ALU.mult), waits=[t_r])
            t_1 = c.op("dve", lambda e, j=j: e.tensor_scalar(out=v[:, j, :], in0=v[:, j, :], scalar1=mv[:, 3:4], scalar2=mv[:, 4:5], op0=ALU.mult, op1=ALU.add), waits=[t_n])
            st["bn_free"] = t_1
            t_2 = c.op("pool", lambda e, j=j: e.tensor_tensor(out=v[:, j, :], in0=v[:, j, :], in1=gc["lng"][:], op=ALU.mult), waits=[t_1])
            t_3 = c.op("pool", lambda e, j=j: e.tensor_tensor(out=v[:, j, :], in0=v[:, j, :], in1=gc["lnb"][:], op=ALU.add), waits=[t_2])
            vn_toks.append(t_3)
        self.chk(23)
        y_toks = []
        tm = None
        for g in range(16):
            b = self.next_bank()
            bk = self.bank(b)
            for j in range(4):
                c.op("pe", lambda e, bk=bk, j=j, g=g: e.matmul(bk[:, j * 128:(j + 1) * 128], lhsT=v[:, j, g * 128:(g + 1) * 128], rhs=gc["wsT"][:, g, :], start=True, stop=False),
                     waits=(vn_toks + [self.bank_free[b]]) if j == 0 else (), sig=False)
                tm = c.op("pe", lambda e, bk=bk, j=j, g=g: e.matmul(bk[:, j * 128:(j + 1) * 128], lhsT=self.ones_b[0:2, 0:128], rhs=gc["bs2"][0:2, g * 128:(g + 1) * 128], start=False, stop=True),
                          sig=(j == 3))
            ta = c.op("dve", lambda e, bk=bk, g=g: e.tensor_tensor(out=yT[:, g, :], in0=bk, in1=uT[:, g, :], op=ALU.mult), waits=[tm, u_toks[g], st.get("yT_free")])
            self.bank_free[b] = ta
            y_toks.append(ta)
        st["v_free"] = tm
        st["uT_free"] = y_toks[-1]
        self.chk(24)
        ytoks = self.proj_a(ws, yT, 16, y_toks)
        st["yT_free"] = ytoks[3]
        self.chk(25)
        self.post_resid(x, x_tok, ytoks, G)

    def alloc_state(self, es, big=True):
        c = self.c
        st = {}
        st["ssq"] = c.sb("ssq", [128, 4], F32, es)
        st["sq"] = c.sb("sq", [128, 4], F32, es)
        st["rstd"] = c.sb("rstd", [128, 4], F32, es)
        st["ssq2"] = c.sb("ssq2", [128, 4], F32, es)
        st["sq2"] = c.sb("sq2", [128, 4], F32, es)
        st["rstd2"] = c.sb("rstd2", [128, 4], F32, es)
        st["xn"] = c.sb("xn", [128, 4, 1024], BF16, es)
        st["hT"] = c.sb("hT", [128, 8, 512], BF16, es)
        st["rtmp"] = [c.sb("rtmp%d" % i, [128, 1024], F32, es) for i in range(2)]
        st["rtmp_free"] = [None, None]
        st["sg"] = [c.sb("sg%d" % i, [128, 512], F32, es) for i in range(2)]
        st["sg_free"] = [None, None]
        if big:
            big = c.sb("big", [128, 32, 512], BF16, es)
            st["big"] = big
            st["uT"] = big[:, 0:16, :]
            st["yT"] = big[:, 16:32, :]
            st["hid"] = big[:, 0:22, :]
        st["epsc"] = c.sb("epsc", [128, 1], F32, es)
        st["t_eps"] = c.op("pool", lambda e: e.memset(st["epsc"][:], EPS))
        self.st = st
        return st

    def phase_A(self, x_in, x_mid, kT_d, v_d, lf_d):
        c = self.c
        NT = self.NT
        din = self.din
        es = ExitStack()
        st = self.alloc_state(es)
        wblocks = {}
        for l in range(2):
            ub, t_in = self.conv_b("win%d" % l, din["a_w_in"][l], 0, 4096)
            ob, t_out = self.conv_a("wout%d" % l, din["a_w_out"][l], GW)
            gb, t_gu = self.conv_gu("wgu%d" % l, din["ffn_w_gu"][l])
            db, t_dn = self.conv_a("wdn%d" % l, din["ffn_w_down"][l], FH)
            wblocks[l] = (ub, t_in, ob, t_out, gb, t_gu, db, t_dn)
        kvb, t_kv = self.conv_b("kvw", din["kv_w"], 0, 2048)
        if self.stop == 1:
            return self._finish(es)
        cols = [c.sb("cols%d" % l, [128, 32], F32, es) for l in range(2)]
        Gt = [[c.sb("G%d_%d" % (l, s), [128, 1024], F32, es) for s in range(2)] for l in range(2)]
        ctoks = []
        for l in range(2):
            pes = ExitStack()
            gains = [din[n][l:l + 1, :] for n in ("pre_mix_g", "post_mix_g", "pre_ffn_g", "post_ffn_g")]
            ctoks += self.layer_mod_consts(pes, "L%d" % l, din["ada_w"][l], din["ada_b"][l:l + 1, :], gains, cols[l], Gt[l])
            c.barrier(self)
            c.flush()
            pes.close()
        if self.stop == 2:
            return self._finish(es)
        pes = ExitStack()
        cols_kv = c.sb("cols_kv", [128, 16], F32, es)
        rowkv = c.sb("modrow_kv", [1, 2 * D], F32, pes, tmp=True)
        tkv = self.mod_row(pes, "kv", din["kv_ada_w"], din["kv_ada_b"].rearrange("(o n) -> o n", o=1), 2 * D, rowkv)
        gkv = c.sb("grow_kv", [1, D], F32, pes, tmp=True)
        tgk = c.dma("sp", "misc", gkv[:], din["kv_norm_g"].rearrange("(o n) -> o n", o=1))
        akv = c.sb("arow_kv", [1, D], F32, pes, tmp=True)
        ta = c.op("dve", lambda e: e.scalar_tensor_tensor(out=akv[:], in0=rowkv[0:1, D:2 * D], scalar=1.0, in1=gkv[:], op0=ALU.add, op1=ALU.mult), waits=tkv + [tgk])
        ctoks.append(self.row_to_cols(akv[:], cols_kv, 0, 8, [ta]))
        ctoks.append(self.row_to_cols(rowkv[0:1, 0:D], cols_kv, 8, 8, tkv))
        c.barrier(self)
        c.flush()
        pes.close()
        if self.stop == 3:
            return self._finish(es)
        gcs = []
        for l in range(2):
            pes = ExitStack()
            gc = {}
            gc["bincol"] = c.sb("bincol%d" % l, [128, 16], F32, es)
            ctoks.append(c.dma("sp", "misc", gc["bincol"][:], din["a_b_in"][l, 0:GW].rearrange("(f p) -> p f", p=128), allow_slow_non_contiguous=True))
            binf = c.sb("binf%d" % l, [1, GW], F32, pes, tmp=True)
            t0 = c.dma("sp", "misc", binf[:], din["a_b_in"][l:l + 1, GW:2 * GW])
            gc["binrow"] = c.sb("binrow%d" % l, [1, GW], BF16, es)
            ctoks.append(c.op("dve", lambda e, gc=gc, binf=binf: e.tensor_copy(out=gc["binrow"][:], in_=binf[:]), waits=[t0]))
            for nm, src in (("lng", "a_ln_g"), ("lnb", "a_ln_b")):
                tmpf = c.sb("lnf_%s%d" % (nm, l), [128, GW], F32, pes, tmp=True)
                t0 = c.dma("sp", "misc", tmpf[:], din[src][l].partition_broadcast(128))
                gc[nm] = c.sb("%s%d" % (nm, l), [128, GW], BF16, es)
                ctoks.append(c.op("dve", lambda e, gc=gc, nm=nm, tmpf=tmpf: e.tensor_copy(out=gc[nm][:], in_=tmpf[:]), waits=[t0]))
            wsf = c.sb("wsf%d" % l, [128, 16, 128], F32, pes, tmp=True)
            t0 = c.dma("sp", "misc", wsf[:], din["a_w_s"][l].rearrange("g t s -> t g s"))
            wsm = c.sb("wsm%d" % l, [128, 16, 128], BF16, pes, tmp=True)
            t1 = c.op("pool", lambda e, wsf=wsf: e.affine_select(out=wsf[:], in_=wsf[:], pattern=[[0, 16], [-1, 128]], compare_op=ALU.is_ge, fill=0.0,
                                                                  base=0, channel_multiplier=1), waits=[t0])
            t2 = c.op("pool", lambda e, wsf=wsf, wsm=wsm: e.tensor_copy(out=wsm[:], in_=wsf[:]), waits=[t1])
            gc["wsT"] = c.sb("wsT%d" % l, [128, 16, 128], BF16, es)
            for g4 in range(4):
                bi = self.next_bank()
                bkb = self.bank(bi).bitcast(BF16)
                tok = None
                for q in range(4):
                    g = g4 * 4 + q
                    tok = c.op("pe", lambda e, bkb=bkb, q=q, g=g, wsm=wsm: e.transpose(out=bkb[:, q * 128:(q + 1) * 128], in_=wsm[:, g, :], identity=self.ident[:]),
                               waits=([t2, self.bank_free[bi]] + self.t_ident) if q == 0 else (), sig=(q == 3))
                te = c.op("dve", lambda e, bkb=bkb, g4=g4, gc=gc: e.tensor_copy(out=gc["wsT"][:, g4 * 4:(g4 + 1) * 4, :], in_=bkb[:, 0:512]), waits=[tok])
                self.bank_free[bi] = te
                ctoks.append(te)
            bsf = c.sb("bsf%d" % l, [1, GW], F32, pes, tmp=True)
            t0 = c.dma("sp", "misc", bsf[:], din["a_b_s"][l:l + 1].rearrange("o g t -> o (g t)"))
            gc["bs2"] = c.sb("bs2_%d" % l, [2, GW], BF16, es)
            bslo = c.sb("bslo%d" % l, [1, GW], BF16, pes, tmp=True)
            bsr = c.sb("bsr%d" % l, [1, GW], F32, pes, tmp=True)
            t1 = c.op("dve", lambda e, gc=gc, bsf=bsf: e.tensor_copy(out=gc["bs2"][0:1, :], in_=bsf[:]), waits=[t0])
            t2 = c.op("dve", lambda e, gc=gc, bsf=bsf, bsr=bsr: e.tensor_tensor(out=bsr[:], in0=bsf[:], in1=gc["bs2"][0:1, :], op=ALU.subtract), waits=[t1])
            t3 = c.op("dve", lambda e, bslo=bslo, bsr=bsr: e.tensor_copy(out=bslo[:], in_=bsr[:]), waits=[t2])
            ctoks.append(t1)
            ctoks.append(c.dma("sp", "misc", gc["bs2"][1:2, :], bslo[:], waits=[t3]))
            gcs.append(gc)
            c.barrier(self)
            c.flush()
            pes.close()
        if self.stop == 4:
            return self._finish(es)
        pes = ExitStack()
        bfb = c.sb("bfb", [128, NH], F32, es)
        ctoks.append(c.dma("sp", "misc", bfb[:], din["kv_b_f"].partition_broadcast(128)))
        kgc = c.sb("kgc", [128, 1], F32, es)
        for e2 in range(2):
            ctoks.append(c.dma("sp", "misc", kgc[e2 * 64:(e2 + 1) * 64, :], din["k_norm_g"].rearrange("(p o) -> p o", o=1), allow_slow_non_contiguous=True))
        wf = c.sb("wf", [128, 8, NH], BF16, es)
        ctoks.append(c.dma("pool", "wf", wf[:], din["kv_w"][:, 2 * D:2 * D + NH].rearrange("(k p) n -> p k n", p=128)))
        vsb = c.sb("vsb", [128, 4, NH, DH + 1], BF16, es)
        t_v1 = c.op("pool", lambda e: e.memset(vsb[:], 1.0))
        onesrow = c.sb("onesrow", [NH, NT * 512], BF16, pes, tmp=True)
        t0 = c.op("pool", lambda e: e.memset(onesrow[:], 1.0))
        ctoks.append(c.dma("sp", "misc", kT_d[:, DH, :], onesrow[:], waits=[t0]))
        ctoks += [st["t_eps"], t_v1]
        c.barrier(self)
        c.flush()
        pes.close()

        if self.stop == 5:
            return self._finish(es)
        x = c.sb("xtile", [128, 4, 1024], F32, es)
        v = c.sb("vbuf", [128, 4, GW], BF16, es)
        st["v"] = v
        st["bnst"] = c.sb("bnst", [128, 4, 6], F32, es)
        st["mv"] = c.sb("mv", [128, 8], F32, es)
        kT_sb = c.sb("kT_sb", [128, 8, 512], BF16, es)
        lf_sb = c.sb("lf_sb", [128, 4, NH], F32, es)
        lft = [c.sb("lft%d" % i, [128, 4, NH], F32, es) for i in range(3)]
        hsq = st["rtmp"][0]
        hst = c.sb("hst", [128, 3, NH], F32, es)

        ws = WStream(c, "A", es)
        plan = []
        for t in range(NT):
            for l in range(2):
                ub, t_in, ob, t_out, gb, t_gu, db, t_dn = wblocks[l]
                plan += [(b, 8, t_in) for b in ub]
                plan += [(b, n, t_out) for (b, n) in ob]
                plan += [(b, 8, t_gu) for b in gb]
                plan += [(b, n, t_dn) for (b, n) in db]
            plan += [(b, 8, t_kv) for b in kvb]
        ws.plan(plan)
        for e in ("pe", "act", "dve", "pool", "sp"):
            c.wait(e, ctoks)
        ws.start()

        x_free = None
        kv_dma_free = [None]
        try:
            self._main_A(ws, x, x_in, x_mid, kT_d, v_d, lf_d, cols, Gt, gcs, cols_kv, kT_sb, lf_sb, lft, hsq, hst, vsb, t_v1, bfb, kgc, wf, es)
        except _Stop:
            return self._finish(es)
        return

    def _main_A(self, ws, x, x_in, x_mid, kT_d, v_d, lf_d, cols, Gt, gcs, cols_kv, kT_sb, lf_sb, lft, hsq, hst, vsb, t_v1, bfb, kgc, wf, es):
        c = self.c
        st = self.st
        NT = self.NT
        x_free = None
        kv_dma_free = [None]
        for t in range(NT):
            tl = c.dma("sp", "xload", x[:], x_in[t * 512:(t + 1) * 512, :].rearrange("(j p) d -> p j d", p=128), waits=[x_free])
            x_tok = [tl] * 4
            for l in range(2):
                self.gmlp(ws, x, x_tok, cols[l], Gt[l][0], gcs[l])
                if self.stop == 6:
                    c.dma("sp", "xstore", x_mid[t * 512:(t + 1) * 512, :].rearrange("(j p) d -> p j d", p=128), x[:], waits=list(x_tok))
                    return self._finish(es)
                self.ffn(ws, x, x_tok, cols[l], Gt[l][1])
                if self.stop == 7:
                    c.dma("sp", "xstore", x_mid[t * 512:(t + 1) * 512, :].rearrange("(j p) d -> p j d", p=128), x[:], waits=list(x_tok))
                    return self._finish(es)
            t_xs = c.dma("sp", "xstore", x_mid[t * 512:(t + 1) * 512, :].rearrange("(j p) d -> p j d", p=128), x[:], waits=list(x_tok))
            hT = st["hT"]
            th = self.norm_mod(x, x_tok, hT, st.get("hT_free"), cols_kv, 0)
            x_free = [t_xs, st["xn_free"]]
            ktoks = [None] * 4
            for hf in range(2):
                bi_, slot, tld = ws.get()
                tok = None
                for j in range(4):
                    b = 2 * j + hf
                    tok = self.mmg(self.bank(b), [(hT[:, k, j * 128:(j + 1) * 128], slot[:, k, :]) for k in range(8)], waits=[tld] + th + [self.bank_free[b]])
                    if hf == 1:
                        ktoks[j] = tok
                ws.release(bi_, tok)
            if self.stop == 11:
                return self._finish(es)
            xn = st["xn"]
            kn_toks = []
            for j in range(4):
                kj = self.PS[:, j * 1024:(j + 1) * 1024]
                t1 = c.op("act", lambda e, kj=kj: e.activation(out=hsq[:], in_=kj, func=AF.Square), waits=[ktoks[j], st.get("hsq_free")])
                t2 = c.op("dve", lambda e: e.tensor_reduce(out=hst[:, 0, :], in_=hsq[:].rearrange("p (h d) -> p h d", d=DH), axis=AX.X, op=ALU.add), waits=[t1])
                st["hsq_free"] = t2
                t3 = c.op("act", lambda e: e.activation(out=hst[:, 1, :], in_=hst[:, 0, :], func=AF.Sqrt, scale=1.0 / DH, bias=st["epsc"][:, 0:1]), waits=[t2])
                t4 = c.op("dve", lambda e: e.reciprocal(out=hst[:, 2, :], in_=hst[:, 1, :]), waits=[t3])
                t5 = c.op("dve", lambda e, j=j, kj=kj: e.tensor_tensor(out=xn[:, j, :].rearrange("p (h d) -> p h d", d=DH), in0=kj.rearrange("p (h d) -> p h d", d=DH),
                                                                      in1=bcast_last(hst[:, 2, :], DH), op=ALU.mult), waits=[t4, st.get("xn_free")])
                self.bank_free[2 * j] = t5
                self.bank_free[2 * j + 1] = t5
                kn_toks.append(t5)
            vtoks = [None] * 4
            for hf in range(2):
                bi_, slot, tld = ws.get()
                tok = None
                for j in range(4):
                    b = 2 * j + hf
                    tok = self.mmg(self.bank(b), [(hT[:, k, j * 128:(j + 1) * 128], slot[:, k, :]) for k in range(8)], waits=[tld] + th + [self.bank_free[b]])
                    if hf == 1:
                        vtoks[j] = tok
                ws.release(bi_, tok)
            vs_toks = []
            for j in range(4):
                vj = self.PS[:, j * 1024:(j + 1) * 1024]
                t1 = c.op("act" if j % 2 == 0 else "dve",
                          (lambda e, j=j, vj=vj: e.activation(out=vsb[:, j, :, 0:DH], in_=vj.rearrange("p (h d) -> p h d", d=DH), func=AF.Copy)) if j % 2 == 0 else
                          (lambda e, j=j, vj=vj: e.tensor_copy(out=vsb[:, j, :, 0:DH], in_=vj.rearrange("p (h d) -> p h d", d=DH))),
                          waits=[vtoks[j], kv_dma_free[0], t_v1])
                self.bank_free[2 * j] = t1
                self.bank_free[2 * j + 1] = t1
                vs_toks.append(t1)
            t_vd = [c.dma("sp", "vstore%d" % j, v_d[:, :, t * 4 + j, :].rearrange("h s e -> s h e"), vsb[:, j, :, :], waits=vs_toks) for j in range(4)]
            if self.stop == 12:
                return self._finish(es)
            kt_toks = []
            tp_last = None
            for kp in range(4):
                bi = self.next_bank()
                bkb = self.bank(bi).bitcast(BF16)
                tok = None
                for kk in range(2):
                    k = 2 * kp + kk
                    for j in range(4):
                        first = (kk == 0 and j == 0)
                        tok = c.op("pe", lambda e, k=k, j=j, kk=kk, bkb=bkb: e.transpose(out=bkb[:, kk * 512 + j * 128:kk * 512 + (j + 1) * 128],
                                                                                         in_=xn[:, j, k * 128:(k + 1) * 128], identity=self.ident[:]),
                                   waits=(kn_toks + [self.bank_free[bi]]) if first else (), sig=(kk == 1 and j == 3))
                tp_last = tok
                te = c.op("act", lambda e, bkb=bkb, kp=kp: e.activation(out=kT_sb[:, 2 * kp:2 * kp + 2, :], in_=bkb[:, 0:1024].rearrange("p (k t) -> p k t", t=512),
                                                                          func=AF.Copy, scale=kgc[:, 0:1]), waits=[tok, kv_dma_free[0]])
                self.bank_free[bi] = te
                kt_toks.append(te)
            st["xn_free"] = tp_last
            x_free.append(tp_last)
            t_kd = []
            for e2 in range(2):
                t_kd.append(c.dma("sp", "kstore%d" % e2, kT_d[:, 0:DH, t * 512:(t + 1) * 512].rearrange("(hc e) d t -> e d hc t", e=2)[e2],
                                  kT_sb[e2 * 64:(e2 + 1) * 64, :, :], waits=kt_toks))
            if self.stop == 13:
                return self._finish(es)
            bi = self.next_bank()
            bk = self.bank(bi)
            tok = None
            for j in range(4):
                tok = self.mmg(bk[:, j * NH:(j + 1) * NH], [(hT[:, k, j * 128:(j + 1) * 128], wf[:, k, :]) for k in range(8)], waits=th + [self.bank_free[bi]], sig=(j == 3))
            st["hT_free"] = tok
            fl, fa, fe = lft
            bk3 = bk[:, 0:4 * NH].rearrange("p (j h) -> p j h", h=NH)
            t1 = c.op("dve", lambda e, bk3=bk3: e.tensor_tensor(out=fl[:], in0=bk3, in1=bcast_mid(bfb[:], 4), op=ALU.add), waits=[tok, kv_dma_free[0]])
            self.bank_free[bi] = t1
            t2 = c.op("act", lambda e: e.activation(out=fa[:], in_=fl[:], func=AF.Abs), waits=[t1])
            t3 = c.op("act", lambda e: e.activation(out=fe[:], in_=fa[:], func=AF.Exp, scale=-1.0), waits=[t2])
            t4 = c.op("act", lambda e: e.activation(out=fa[:], in_=fe[:], func=AF.Ln, bias=1.0), waits=[t3])
            t5 = c.op("dve", lambda e: e.tensor_scalar(out=fe[:], in0=fl[:], scalar1=0.0, scalar2=None, op0=ALU.min), waits=[t4])
            t6 = c.op("dve", lambda e: e.tensor_tensor(out=lf_sb[:], in0=fe[:], in1=fa[:], op=ALU.subtract), waits=[t5])
            t_ld = c.dma("sp", "lfstore", lf_d[:, t * 4:(t + 1) * 4, :], lf_sb[:], waits=[t6])
            kv_dma_free[0] = t_vd + [t_ld] + t_kd
        c.wait("sp", [t_xs, kv_dma_free[0]])
        c.barrier(self)
        c.flush()
        es.close()


WEIGHT_SPECS_A = [
    ("ada_w", (2, D, 6 * D)), ("ada_b", (2, 6 * D)), ("pre_mix_g", (2, D)), ("post_mix_g", (2, D)), ("pre_ffn_g", (2, D)), ("post_ffn_g", (2, D)),
    ("ffn_w_gu", (2, D, 2 * FH)), ("ffn_w_down", (2, FH, D)), ("a_w_in", (2, D, 2 * GW)), ("a_b_in", (2, 2 * GW)), ("a_ln_g", (2, GW)), ("a_ln_b", (2, GW)),
    ("a_w_s", (2, 16, 128, 128)), ("a_b_s", (2, 16, 128)), ("a_w_out", (2, GW, D)), ("kv_ada_w", (D, 2 * D)), ("kv_ada_b", (2 * D,)), ("kv_norm_g", (D,)),
    ("kv_w", (D, 2 * D + NH)), ("kv_b_f", (NH,)), ("k_norm_g", (DH,)),
]


def build_A(NT):
    p = Prog(NT, 0, "A")
    p.inp("x_sh", (NT * 512, D))
    p.inp("c_col", (128, 8))
    for n, s in WEIGHT_SPECS_A:
        p.inp(n, s)
    x_mid = p.outp("x_mid", (NT * 512, D))
    kT_d = p.outp("kT_d", (NH, DH + 1, NT * 512), BF16)
    v_d = p.outp("v_d", (NH, 128, NT * 4, DH + 1), BF16)
    lf_d = p.outp("lf_d", (128, NT * 4, NH))
    p.setup_common()
    p.phase_A(p.din["x_sh"], x_mid, kT_d, v_d, lf_d)
    p.es.close()
    return p


def zigzag(nsb, p):
    out = []
    for i in range(nsb // 2):
        out.append(2 * i + ((i + p) % 2))
    return out


def _phase_B(self, x_mid, out_d, kT_all, v_all, lf_all, maskd, flagd):
    c = self.c
    NT = self.NT
    NKB = 8 * NT
    din = self.din
    es = ExitStack()
    st = self.alloc_state(es, big=False)
    sbs = [zigzag(2 * NT, 0), zigzag(2 * NT, 1)]
    wblocks = {}
    for l in range(2):
        qb, t_q = self.conv_b("wq%d" % l, din["b_w_qg"][l], 0, D)
        gb_, t_g = self.conv_b("wg%d" % l, din["b_w_qg"][l], D, D)
        ob, t_o = self.conv_a("wo%d" % l, din["b_w_o"][l], D)
        gu, t_gu = self.conv_gu("wguB%d" % l, din["ffn_w_gu"][l])
        db, t_dn = self.conv_a("wdnB%d" % l, din["ffn_w_down"][l], FH)
        wblocks[l] = (qb, t_q, gb_, t_g, ob, t_o, gu, t_gu, db, t_dn)
    cols = [c.sb("colsB%d" % l, [128, 32], F32, es) for l in range(2)]
    Gt = [[c.sb("GB%d_%d" % (l, s), [128, 1024], F32, es) for s in range(2)] for l in range(2)]
    for l in range(2):
        pes = ExitStack()
        gains = [din[n][l:l + 1, :] for n in ("pre_mix_g", "post_mix_g", "pre_ffn_g", "post_ffn_g")]
        self.layer_mod_consts(pes, "LB%d" % l, din["ada_w"][l], din["ada_b"][l:l + 1, :], gains, cols[l], Gt[l])
        c.barrier(self)
        c.flush()
        pes.close()
    qgc = c.sb("qgc", [64, 2], F32, es)
    t0 = c.dma("sp", "misc", qgc[:], din["b_q_norm_g"].rearrange("l p -> p l"), allow_slow_non_contiguous=True)
    c.op("dve", lambda e: e.tensor_scalar(out=qgc[:], in0=qgc[:], scalar1=DH ** -0.5, scalar2=None, op0=ALU.mult), waits=[t0])
    masks = c.sb("masks", [128, 2, 4, 512], BF16, es)
    c.dma("sp", "misc", masks[:], maskd.rearrange("r k p t -> p r k t"))
    flags = c.sb("flags", [128, 8], F32, es)
    c.dma("sp", "misc", flags[:], flagd)
    pes = ExitStack()
    Dg = c.sb("Dg", [128, NKB, NH], F32, es)
    lfg = c.sb("lfg", [128, NKB, NH], F32, pes, tmp=True)
    sc = [c.sb("scan%d" % i, [128, NKB, NH], F32, pes, tmp=True) for i in range(2)]
    U = c.sb("Utri", [128, 128], F32, pes, tmp=True)
    tl = []
    for r in range(2):
        for li, sbi in enumerate(sbs[r]):
            tl.append(c.dma("sp", "lfl%d" % ((r * NT + li) % 4), lfg[:, 4 * sbi:4 * sbi + 4, :], lf_all[r, :, 4 * li:4 * li + 4, :]))
    tu0 = c.op("pool", lambda e: e.memset(U[:], 1.0))
    tu = c.op("pool", lambda e: e.affine_select(out=U[:], in_=U[:], pattern=[[1, 128]], compare_op=ALU.is_ge, fill=0.0, base=0, channel_multiplier=-1), waits=[tu0])
    cur = lfg
    tprev = tl
    step = 1
    k = 0
    while step < NKB:
        nxt = sc[k % 2]
        ta = c.op("dve", lambda e, cur=cur, nxt=nxt, step=step: e.tensor_copy(out=nxt[:, 0:step, :], in_=cur[:, 0:step, :]), waits=tprev)
        tb = c.op("dve", lambda e, cur=cur, nxt=nxt, step=step: e.tensor_tensor(out=nxt[:, step:NKB, :], in0=cur[:, step:NKB, :], in1=cur[:, 0:NKB - step, :], op=ALU.add), waits=tprev + [ta])
        tprev = [ta, tb]
        cur = nxt
        step *= 2
        k += 1
    ex = sc[k % 2]
    te = c.op("dve", lambda e: e.tensor_tensor(out=ex[:], in0=cur[:], in1=lfg[:], op=ALU.subtract), waits=tprev + tl)
    ncol = NKB * NH
    tds = []
    for h0 in range(0, ncol, 512):
        w = min(512, ncol - h0)
        bi = self.next_bank()
        bk = self.bank(bi)[:, 0:w]
        lf2 = lfg[:].rearrange("p k h -> p (k h)")[:, h0:h0 + w]
        ex2 = ex[:].rearrange("p k h -> p (k h)")[:, h0:h0 + w]
        tm = self.mmg(bk, [(U[:], lf2), (self.ones_f[:], ex2)], waits=[tu, te, self.bank_free[bi]] + tl + self.t_ones)
        td = c.op("dve", lambda e, bk=bk, h0=h0, w=w: e.tensor_copy(out=Dg[:].rearrange("p k h -> p (k h)")[:, h0:h0 + w], in_=bk), waits=[tm])
        self.bank_free[bi] = td
        tds.append(td)
    c.barrier(self)
    c.flush()
    pes.close()

    x = c.sb("xtileB", [128, 4, 1024], F32, es)
    hid = c.sb("hidB", [128, 22, 512], BF16, es)
    st["hid"] = hid
    QT = c.sb("QT", [DH + 1, NH, 512], BF16, es)
    gate = hid[:, 0:8, :].rearrange("p (j a) t -> p j (a t)", a=2)
    og = st["xn"]
    kTb = [c.sb("kTb%d" % i, [DH + 1, 2, NT * 512], BF16, es) for i in range(2)]
    vb = [c.sb("vb%d" % i, [128, 2, NT * 4, DH + 1], BF16, es) for i in range(2)]
    pt = [c.sb("pt%d" % i, [128, 512], BF16, es) for i in range(4)]
    oT = [c.sb("oT%d" % i, [DH + 1, 512], F32, es) for i in range(2)]
    biasT = c.sb("biasT", [128, 2, NT * 4, NH], F32, es)
    Dq = c.sb("Dq", [128, 4, NH], F32, es)
    Dtmp = c.sb("Dtmp", [128, 4, NH], F32, es)
    Drefb = c.sb("Drefb", [128, NH], F32, es)
    cD = c.sb("cD", [128, 4, NH], F32, es)
    crow = c.sb("crow", [NH, 512], BF16, es)
    rinv = c.sb("rinv", [128, 4], F32, es)
    hst = c.sb("hstB", [128, 3, NH], F32, es)
    hsq = st["rtmp"][0]

    ws = WStream(c, "B", es, nbuf=3)
    plan = []
    for t in range(NT):
        for l in range(2):
            qb, t_q, gb_, t_g, ob, t_o, gu, t_gu, db, t_dn = wblocks[l]
            plan += [(b, 8, t_q) for b in qb]
            plan += [(b, 8, t_g) for b in gb_]
            plan += [(b, n, t_o) for (b, n) in ob]
            plan += [(b, 8, t_gu) for b in gu]
            plan += [(b, n, t_dn) for (b, n) in db]
    ws.plan(plan)
    ws.start()

    S_BANKS = [0, 1, 2]
    O_BANKS = [3, 4]
    x_free = None
    kv_free = [None, None]
    pt_free = [None] * 4
    oT_free = [None, None]
    hcount = 0
    ptc = 0
    t_os = None
    for t in range(NT):
        i = t
        tl_ = c.dma("sp", "xloadB", x[:], x_mid[t * 512:(t + 1) * 512, :].rearrange("(j p) d -> p j d", p=128), waits=[x_free])
        x_tok = [tl_] * 4
        g0, g1 = 4 * sbs[0][i], 4 * sbs[1][i]
        t1 = c.op("dve", lambda e, g0=g0: e.tensor_scalar(out=Dtmp[:], in0=Dg[:, g0:g0 + 4, :], scalar1=flags[:, 0:1], scalar2=None, op0=ALU.mult), waits=[st.get("D_free")])
        t2 = c.op("dve", lambda e, g1=g1: e.scalar_tensor_tensor(out=Dq[:], in0=Dg[:, g1:g1 + 4, :], scalar=flags[:, 1:2], in1=Dtmp[:], op0=ALU.mult, op1=ALU.add), waits=[t1])
        bi = self.next_bank()
        bk = self.bank(bi)
        tm = c.op("pe", lambda e, bk=bk: e.matmul(bk[:, 0:NH], lhsT=self.ones_f[0:1, 0:128], rhs=Dq[0:1, 0, :], start=True, stop=True), waits=[t2, self.bank_free[bi]])
        t3 = c.op("dve", lambda e, bk=bk: e.tensor_copy(out=Drefb[:], in_=bk[:, 0:NH]), waits=[tm])
        self.bank_free[bi] = t3
        t4 = c.op("dve", lambda e: e.tensor_tensor(out=cD[:], in0=Dq[:], in1=bcast_mid(Drefb[:], 4), op=ALU.subtract), waits=[t3])
        bi = self.next_bank()
        bk = self.bank(bi)
        tp = None
        for j in range(4):
            tp = c.op("pe", lambda e, bk=bk, j=j: e.transpose(out=bk[0:NH, j * 128:(j + 1) * 128], in_=cD[:, j, :], identity=self.identf[:]),
                      waits=[t4, self.bank_free[bi]] + self.t_ident if j == 0 else (), sig=(j == 3))
        t5 = c.op("dve", lambda e, bk=bk: e.tensor_copy(out=crow[:], in_=bk[0:NH, :]), waits=[tp, st.get("crow_free")])
        self.bank_free[bi] = t5
        t_cr = c.dma("sp", "crowd", QT[DH:DH + 1, :, :], crow[:], waits=[t5, st.get("QT_free")])
        st["crow_free"] = t_cr
        tbias = []
        for r in range(2):
            for li in range(i + 1):
                gk = 4 * sbs[r][li]
                neg = flags[:, 2 + 2 * r + (i % 2):3 + 2 * r + (i % 2)]
                if li == i:
                    tbias.append(c.op("dve", lambda e, r=r, li=li, gk=gk, neg=neg: e.scalar_tensor_tensor(
                        out=biasT[:, r, 4 * li:4 * li + 4, :], in0=bcast_mid(Drefb[:], 4), scalar=neg, in1=Dg[:, gk:gk + 4, :], op0=ALU.add, op1=ALU.subtract),
                        waits=[t3, st.get("bias_free")]))
                else:
                    tbias.append(c.op("dve", lambda e, r=r, li=li, gk=gk: e.tensor_tensor(
                        out=biasT[:, r, 4 * li:4 * li + 4, :], in0=bcast_mid(Drefb[:], 4), in1=Dg[:, gk:gk + 4, :], op=ALU.subtract),
                        waits=[t3, st.get("bias_free")]))
        st["D_free"] = tbias[-1]
        nkl = 4 * (i + 1)
        for l in range(2):
            hT = st["hT"]
            th = self.norm_mod(x, x_tok, hT, st.get("hT_free"), cols[l], 0)
            qtoks = [None] * 4
            for hf in range(2):
                bi_, slot, tld = ws.get()
                tok = None
                for j in range(4):
                    b = 2 * j + hf
                    tok = self.mmg(self.bank(b), [(hT[:, k, j * 128:(j + 1) * 128], slot[:, k, :]) for k in range(8)], waits=[tld] + th + [self.bank_free[b]])
                    if hf == 1:
                        qtoks[j] = tok
                ws.release(bi_, tok)
            xn = st["xn"]
            qn_toks = []
            for j in range(4):
                qj = self.PS[:, j * 1024:(j + 1) * 1024]
                a1 = c.op("act", lambda e, qj=qj: e.activation(out=hsq[:], in_=qj, func=AF.Square), waits=[qtoks[j], st.get("hsq_free"), st["rtmp_free"][0]])
                a2 = c.op("dve", lambda e: e.tensor_reduce(out=hst[:, 0, :], in_=hsq[:].rearrange("p (h d) -> p h d", d=DH), axis=AX.X, op=ALU.add), waits=[a1])
                st["hsq_free"] = a2
                st["rtmp_free"][0] = a2
                a3 = c.op("act", lambda e: e.activation(out=hst[:, 1, :], in_=hst[:, 0, :], func=AF.Sqrt, scale=1.0 / DH, bias=st["epsc"][:, 0:1]), waits=[a2])
                a4 = c.op("dve", lambda e: e.reciprocal(out=hst[:, 2, :], in_=hst[:, 1, :]), waits=[a3])
                a5 = c.op("dve", lambda e, j=j, qj=qj: e.tensor_tensor(out=xn[:, j, :].rearrange("p (h d) -> p h d", d=DH), in0=qj.rearrange("p (h d) -> p h d", d=DH),
                                                                      in1=bcast_last(hst[:, 2, :], DH), op=ALU.mult), waits=[a4, st.get("xn_free")])
                self.bank_free[2 * j] = a5
                self.bank_free[2 * j + 1] = a5
                qn_toks.append(a5)
            gtoks = [None] * 4
            tok = None
            for hf in range(2):
                bi_, slot, tld = ws.get()
                for j in range(4):
                    b = 2 * j + hf
                    tok = self.mmg(self.bank(b), [(hT[:, k, j * 128:(j + 1) * 128], slot[:, k, :]) for k in range(8)], waits=[tld] + th + [self.bank_free[b]])
                    if hf == 1:
                        gtoks[j] = tok
                ws.release(bi_, tok)
            st["hT_free"] = tok
            g_toks = []
            for j in range(4):
                gj = self.PS[:, j * 1024:(j + 1) * 1024]
                a1 = c.op("act", lambda e, j=j, gj=gj: e.activation(out=gate[:, j, :], in_=gj, func=AF.Sigmoid), waits=[gtoks[j], st.get("gate_free")])
                self.bank_free[2 * j] = a1
                self.bank_free[2 * j + 1] = a1
                g_toks.append(a1)
            qt_toks = []
            tp_last = None
            for hp in range(8):
                bi = self.next_bank()
                bkb = self.bank(bi).bitcast(BF16)
                tok = None
                for e2 in range(2):
                    h = 2 * hp + e2
                    for j in range(4):
                        first = (e2 == 0 and j == 0)
                        tok = c.op("pe", lambda e, h=h, j=j, e2=e2, bkb=bkb: e.transpose(out=bkb[0:DH, e2 * 512 + j * 128:e2 * 512 + (j + 1) * 128],
                                                                                         in_=xn[:, j, h * DH:(h + 1) * DH], identity=self.ident[:]),
                                   waits=(qn_toks + [self.bank_free[bi]]) if first else (), sig=(e2 == 1 and j == 3))
                tp_last = tok
                te_ = c.op("act", lambda e, bkb=bkb, hp=hp, l=l: e.activation(out=QT[0:DH, 2 * hp:2 * hp + 2, :], in_=bkb[0:DH, 0:1024].rearrange("p (k t) -> p k t", t=512),
                                                                            func=AF.Copy, scale=qgc[:, l:l + 1]), waits=[tok, st.get("QT_free")])
                self.bank_free[bi] = te_
                qt_toks.append(te_)
            st["xn_free"] = tp_last
            og_toks = []
            last_s = None
            for h in range(NH):
                kb_ = hcount % 2
                hcount += 1
                tk = []
                for r in range(2):
                    tk.append(c.dma("sp", "kld%d_%d" % (kb_, r), kTb[kb_][:, r, 0:nkl * 128], kT_all[r, h, :, 0:nkl * 128], waits=[kv_free[kb_]]))
                    tk.append(c.dma("sp", "vld%d_%d" % (kb_, r), vb[kb_][:, r, 0:nkl, :], v_all[r, h, :, 0:nkl, :], waits=[kv_free[kb_]]))
                ob = O_BANKS[h % 2]
                obk = self.bank(ob)[0:DH + 1, :]
                nblk = 2 * nkl
                pv = None
                blks = [(r, kb) for r in range(2) for kb in range(nkl)]
                use_tok = {}
                slot_of = {}
                LOOK = 2
                for it in range(nblk + LOOK):
                    if it < nblk:
                        idx = it
                        r, kb = blks[idx]
                        sbk = S_BANKS[idx % 3]
                        ts_ = c.op("pe", lambda e, sbk=sbk, kb_=kb_, r=r, kb=kb, h=h: e.matmul(self.bank(sbk), lhsT=kTb[kb_][:, r, kb * 128:(kb + 1) * 128], rhs=QT[:, h, :], start=True, stop=True),
                                   waits=tk + qt_toks + [t_cr, self.bank_free[sbk]] if idx == 0 else [self.bank_free[sbk]])
                        ps_ = ptc % 4
                        ptc += 1
                        if kb >= nkl - 4:
                            ts_ = c.op("dve", lambda e, sbk=sbk, r=r, kk=kb - (nkl - 4): e.tensor_tensor(out=self.bank(sbk), in0=self.bank(sbk), in1=masks[:, r, kk, :], op=ALU.add), waits=[ts_])
                        ta_ = c.op("act", lambda e, sbk=sbk, ps_=ps_, r=r, kb=kb, h=h: e.activation(out=pt[ps_][:], in_=self.bank(sbk), func=AF.Exp, bias=biasT[:, r, kb, h:h + 1]),
                                   waits=[ts_, pt_free[ps_]] + (tbias if idx == 0 else []))
                        self.bank_free[sbk] = ta_
                        use_tok[idx] = ta_
                        slot_of[idx] = ps_
                    jdx = it - LOOK
                    if jdx >= 0:
                        r, kb = blks[jdx]
                        ps_ = slot_of[jdx]
                        pv = c.op("pe", lambda e, obk=obk, kb_=kb_, r=r, kb=kb, ps_=ps_, jdx=jdx, nblk=nblk: e.matmul(obk, lhsT=vb[kb_][:, r, kb, :], rhs=pt[ps_][:], start=(jdx == 0), stop=(jdx == nblk - 1)),
                                  waits=[use_tok[jdx]] + ([self.bank_free[ob]] if jdx == 0 else []))
                        pt_free[ps_] = pv
                kv_free[kb_] = pv
                last_s = pv
                osl = h % 2
                tcp = c.op("dve", lambda e, obk=obk, osl=osl: e.tensor_copy(out=oT[osl][:], in_=obk), waits=[pv, oT_free[osl]])
                self.bank_free[ob] = tcp
                bi = 5 + (h % 3)
                bk = self.bank(bi)
                tp = None
                for j in range(4):
                    tp = c.op("pe", lambda e, bk=bk, j=j, osl=osl: e.transpose(out=bk[:, j * (DH + 1):(j + 1) * (DH + 1)], in_=oT[osl][:, j * 128:(j + 1) * 128], identity=self.identf[0:DH + 1, 0:DH + 1]),
                              waits=[tcp, self.bank_free[bi]] if j == 0 else (), sig=(j == 3))
                oT_free[osl] = tp
                bk3 = bk[:, 0:4 * (DH + 1)].rearrange("p (j e) -> p j e", e=DH + 1)
                tr = c.op("dve", lambda e, bk3=bk3: e.reciprocal(out=rinv[:].rearrange("p (j o) -> p j o", o=1), in_=bk3[:, :, DH:DH + 1]), waits=[tp, st.get("rinv_free")])
                tg_ = None
                for j in range(4):
                    tg_ = c.op("dve", lambda e, bk3=bk3, j=j, h=h: e.scalar_tensor_tensor(out=og[:, j, h * DH:(h + 1) * DH], in0=bk3[:, j, 0:DH], scalar=rinv[:, j:j + 1],
                                                                                       in1=gate[:, j, h * DH:(h + 1) * DH], op0=ALU.mult, op1=ALU.mult),
                               waits=[tr] + g_toks + [st.get("og_free"), st.get("xn_free")])
                st["rinv_free"] = tg_
                self.bank_free[bi] = tg_
                og_toks.append(tg_)
            st["QT_free"] = last_s
            st["bias_free"] = last_s
            st["gate_free"] = og_toks[-1]
            ogT = st["hT"]
            tt = []
            tp_last = None
            for kp in range(4):
                bi = self.next_bank()
                bkb = self.bank(bi).bitcast(BF16)
                tok = None
                for kk in range(2):
                    k = 2 * kp + kk
                    for j in range(4):
                        first = (kk == 0 and j == 0)
                        tok = c.op("pe", lambda e, k=k, j=j, kk=kk, bkb=bkb: e.transpose(out=bkb[:, kk * 512 + j * 128:kk * 512 + (j + 1) * 128],
                                                                                         in_=og[:, j, k * 128:(k + 1) * 128], identity=self.ident[:]),
                                   waits=(og_toks + [self.bank_free[bi]]) if first else (), sig=(kk == 1 and j == 3))
                tp_last = tok
                te_ = c.op("act", lambda e, bkb=bkb, kp=kp: e.activation(out=ogT[:, 2 * kp:2 * kp + 2, :], in_=bkb[:, 0:1024].rearrange("p (k t) -> p k t", t=512), func=AF.Copy),
                           waits=[tok, st.get("hT_free")])
                self.bank_free[bi] = te_
                tt.append(te_)
            st["og_free"] = tp_last
            st["xn_free"] = tp_last
            ytoks = self.proj_a(ws, ogT, 8, tt)
            st["hT_free"] = ytoks[3]
            self.post_resid(x, x_tok, ytoks, Gt[l][0])
            self.ffn(ws, x, x_tok, cols[l], Gt[l][1])
        t_os = c.dma("sp", "ostore", out_d[t * 512:(t + 1) * 512, :].rearrange("(j p) d -> p j d", p=128), x[:], waits=list(x_tok))
        x_free = [t_os]
    c.wait("sp", [t_os])
    c.barrier(self)
    c.flush()
    es.close()


Prog.phase_B = _phase_B

WEIGHT_SPECS_B = [
    ("ada_w", (2, D, 6 * D)), ("ada_b", (2, 6 * D)), ("pre_mix_g", (2, D)), ("post_mix_g", (2, D)), ("pre_ffn_g", (2, D)), ("post_ffn_g", (2, D)),
    ("ffn_w_gu", (2, D, 2 * FH)), ("ffn_w_down", (2, FH, D)), ("b_w_qg", (2, D, 2 * D)), ("b_q_norm_g", (2, DH)), ("b_w_o", (2, D, D)),
]


def build_B(NT):
    p = Prog(NT, 0, "B")
    p.inp("x_mid", (NT * 512, D))
    p.inp("c_col", (128, 8))
    p.inp("kT_all", (2, NH, DH + 1, NT * 512), BF16)
    p.inp("v_all", (2, NH, 128, NT * 4, DH + 1), BF16)
    p.inp("lf_all", (2, 128, NT * 4, NH))
    p.inp("maskd", (2, 4, 128, 512), BF16)
    p.inp("flagd", (128, 8))
    for n, s in WEIGHT_SPECS_B:
        p.inp(n, s)
    out = p.outp("out", (NT * 512, D))
    p.setup_common()
    p.phase_B(p.din["x_mid"], out, p.din["kT_all"], p.din["v_all"], p.din["lf_all"], p.din["maskd"], p.din["flagd"])
    p.es.close()
    return p


def core_consts(p):
    import ml_dtypes
    m = np.zeros((2, 4, 128, 512), np.float32)
    s = np.arange(128)[:, None]
    t = np.arange(512)[None, :]
    for k in range(4):
        m[p, k] = np.where(t >= 128 * k + s, 0.0, -30000.0)
    f = np.zeros((128, 8), np.float32)
    f[:, p] = 1.0
    NEG = -30000.0
    for par in range(2):
        f[:, 2 + 2 * (1 - p) + par] = 0.0 if ((par + p) % 2 == 1) else NEG
    return m.astype(ml_dtypes.bfloat16), f


_LAYERED = ("ada_w", "ada_b", "pre_mix_g", "post_mix_g", "pre_ffn_g", "post_ffn_g", "ffn_w_gu", "ffn_w_down")
_CACHE = {}


def _tokens(NT, p):
    return np.concatenate([np.arange(q * 512, (q + 1) * 512) for q in zigzag(2 * NT, p)])


def kernel(**inputs):
    inp = {k: np.asarray(v) for k, v in inputs.items()}
    x = inp["x"]
    B, S, _ = x.shape
    NT = S // 1024
    ncores = 2 * B
    toks = [_tokens(NT, 0), _tokens(NT, 1)]
    if ("A", NT) not in _CACHE:
        _CACHE[("A", NT)] = build_A(NT)
    pA = _CACHE[("A", NT)]
    mapsA = []
    for core in range(ncores):
        b, p = core // 2, core % 2
        m = {"x_sh": np.ascontiguousarray(x[b, toks[p]]), "c_col": np.ascontiguousarray(inp["c"][b].reshape(8, 128).T)}
        for n, s in WEIGHT_SPECS_A:
            m[n] = np.ascontiguousarray(inp[n][0:2]) if n in _LAYERED else inp[n]
        mapsA.append(m)
    resA = run_bass_kernel_spmd(pA.nc, mapsA, core_ids=list(range(ncores))).results
    if ("B", NT) not in _CACHE:
        _CACHE[("B", NT)] = build_B(NT)
    pB = _CACHE[("B", NT)]
    mapsB = []
    for core in range(ncores):
        b, p = core // 2, core % 2
        mk, fl = core_consts(p)
        r0, r1 = resA[2 * b], resA[2 * b + 1]
        m = {"x_mid": np.asarray(resA[core]["x_mid"]), "c_col": mapsA[core]["c_col"],
             "kT_all": np.stack([np.asarray(r0["kT_d"]), np.asarray(r1["kT_d"])]),
             "v_all": np.stack([np.asarray(r0["v_d"]), np.asarray(r1["v_d"])]),
             "lf_all": np.stack([np.asarray(r0["lf_d"]), np.asarray(r1["lf_d"])]),
             "maskd": mk, "flagd": fl}
        for n, s in WEIGHT_SPECS_B:
            m[n] = np.ascontiguousarray(inp[n][2:4]) if n in _LAYERED else inp[n]
        mapsB.append(m)
    resB = run_bass_kernel_spmd(pB.nc, mapsB, core_ids=list(range(ncores))).results
    out = np.empty((B, S, D), np.float32)
    for core in range(ncores):
        b, p = core // 2, core % 2
        out[b, toks[p]] = np.asarray(resB[core]["out"])
    return out
```
